# Optimizing a Trainium2 kernel written in Bass

```python
import jax, jax.numpy as jnp
from jax import lax
import numpy as np

D_MODEL = 2048
BATCH = 8
SEQ = 2048
DEPTH = 2

GRID_W = 64
CTX_LEN = 256
N_MIXERS = 2
N_SUB = 3
D_FF = 5632
MACARON = 0.5
LRU_WIDTH = D_MODEL
LRU_HEADS = 16
LRU_BLOCK = LRU_WIDTH // LRU_HEADS
CONV_W = 4
RG_C = 8.0
N_FGROUPS = 8
FGROUP = D_MODEL // N_FGROUPS
N_REC = (DEPTH + 1) // 2
N_FOU = DEPTH // 2
EPS = 1e-6

kernel_name = 'hybrid_rglru_fnet_macaron_dit'


def rmsnorm(x, g):
    xf = x.astype(jnp.float32)
    y = xf * lax.rsqrt(jnp.mean(xf * xf, axis=-1, keepdims=True) + EPS)
    return (y * g.astype(jnp.float32)).astype(x.dtype)


def modulate(h, shift, scale):
    return h * (1 + scale) + shift


def adaln(cvec, w, b):
    m = (jax.nn.silu(cvec) @ w + b)[:, None, :]
    return jnp.split(m, 3 * N_SUB, axis=-1)


def swiglu(h, w_in, w_out):
    gate, up = jnp.split(h @ w_in, 2, axis=-1)
    return (jax.nn.silu(gate) * up) @ w_out


def ffn_sublayer(x, g_pre, g_post, shift, scale, gate, w_in, w_out):
    y = swiglu(modulate(rmsnorm(x, g_pre), shift, scale), w_in, w_out)
    return x + MACARON * gate * rmsnorm(y, g_post)


def centred_dwconv(v, w, b, axis):
    n = v.shape[axis]
    left = CONV_W // 2
    pad = [(0, 0)] * v.ndim
    pad[axis] = (left, CONV_W - 1 - left)
    vp = jnp.pad(v, pad)
    out = b
    for k in range(CONV_W):
        out = out + w[k] * lax.slice_in_dim(vp, k, k + n, axis=axis)
    return out


def block_diag(v, w, b):
    bsz, n, r = v.shape
    y = jnp.einsum('blhi,hij->blhj', v.reshape(bsz, n, LRU_HEADS, LRU_BLOCK), w)
    return y.reshape(bsz, n, r) + b


def rglru_coeffs(v, wa, ba, wx, bx, lam):
    r = jax.nn.sigmoid(block_diag(v, wa, ba)).astype(jnp.float32)
    i = jax.nn.sigmoid(block_diag(v, wx, bx)).astype(jnp.float32)
    log_a = -RG_C * r * jax.nn.softplus(-lam.astype(jnp.float32))
    a = jnp.exp(log_a)
    mult = jnp.sqrt(-jnp.expm1(2.0 * log_a))
    return a, mult * i * v.astype(jnp.float32)


def linear_scan(a, b, h0, reverse):
    def comb(lhs, rhs):
        a1, b1 = lhs
        a2, b2 = rhs
        return a1 * a2, a2 * b1 + b2
    a_cum, b_cum = lax.associative_scan(comb, (a, b), axis=1, reverse=reverse)
    return a_cum * h0[:, None, :] + b_cum


def rglru_mixer(h_lat, h_ctx, rows, w_in, conv_w, conv_b, gate_w, gate_b, lam, w_out, need_ctx_out):
    bsz, n, _ = h_lat.shape
    g_lat, v_lat = jnp.split(h_lat @ w_in, 2, axis=-1)
    v_lat = centred_dwconv(v_lat.reshape(bsz, rows, GRID_W, LRU_WIDTH), conv_w, conv_b, axis=2)
    v_lat = v_lat.reshape(bsz, n, LRU_WIDTH)
    v_ctx = centred_dwconv(h_ctx @ w_in[:, LRU_WIDTH:], conv_w, conv_b, axis=1)
    zeros = jnp.zeros((bsz, LRU_WIDTH), jnp.float32)
    y_lat = jnp.zeros((bsz, n, LRU_WIDTH), jnp.float32)
    y_ctx = jnp.zeros(v_ctx.shape, jnp.float32)
    for d, reverse in enumerate((False, True)):
        a_c, b_c = rglru_coeffs(v_ctx, gate_w[d, 0], gate_b[d, 0], gate_w[d, 1], gate_b[d, 1], lam[d])
        h_c = linear_scan(a_c, b_c, zeros, reverse)
        h_end = h_c[:, 0] if reverse else h_c[:, -1]
        a_l, b_l = rglru_coeffs(v_lat, gate_w[d, 0], gate_b[d, 0], gate_w[d, 1], gate_b[d, 1], lam[d])
        y_lat = y_lat + linear_scan(a_l, b_l, h_end, reverse)
        if need_ctx_out:
            y_ctx = y_ctx + h_c
    out_lat = (jax.nn.gelu(g_lat) * y_lat.astype(h_lat.dtype)) @ w_out
    out_ctx = None
    if need_ctx_out:
        g_ctx = h_ctx @ w_in[:, :LRU_WIDTH]
        out_ctx = (jax.nn.gelu(g_ctx) * y_ctx.astype(h_ctx.dtype)) @ w_out
    return out_lat, out_ctx


def fourier_mixer(h, w_out):
    bsz, n, _ = h.shape
    hg = h.reshape(bsz, n, N_FGROUPS, FGROUP).astype(jnp.float32)
    f = jnp.fft.fftn(hg, axes=(1, 3), norm='ortho').real
    return f.reshape(bsz, n, D_MODEL).astype(h.dtype) @ w_out


def setup_inputs(seed: int = 0) -> dict:
    key = jax.random.key(seed)
    ks = jax.random.split(key, 20)
    f32 = jnp.float32
    x = jax.random.normal(ks[0], (BATCH, SEQ, D_MODEL), f32)
    c = jax.random.normal(ks[1], (BATCH, D_MODEL), f32)
    ctx = jax.random.normal(ks[2], (BATCH, CTX_LEN, D_MODEL), f32)
    c_ctx = jax.random.normal(ks[3], (D_MODEL,), f32)
    mod_w = jax.random.normal(ks[4], (DEPTH, D_MODEL, 3 * N_SUB * D_MODEL), f32) * (0.5 * D_MODEL ** -0.5)
    mod_b = jax.random.normal(ks[5], (DEPTH, 3 * N_SUB * D_MODEL), f32) * 0.01
    norm_g = 1.0 + 0.01 * jax.random.normal(ks[6], (DEPTH, 2 * N_SUB, D_MODEL), f32)
    ffn_w_in = jax.random.normal(ks[7], (DEPTH, 2, D_MODEL, 2 * D_FF), f32) * D_MODEL ** -0.5
    ffn_w_out = jax.random.normal(ks[8], (DEPTH, 2, D_FF, D_MODEL), f32) * D_FF ** -0.5
    rec_w_in = jax.random.normal(ks[9], (N_REC, D_MODEL, 2 * LRU_WIDTH), f32) * D_MODEL ** -0.5
    rec_conv_w = jax.random.normal(ks[10], (N_REC, CONV_W, LRU_WIDTH), f32) * CONV_W ** -0.5
    rec_conv_b = jax.random.normal(ks[11], (N_REC, LRU_WIDTH), f32) * 0.01
    rec_gate_w = jax.random.normal(ks[12], (N_REC, 2, 2, LRU_HEADS, LRU_BLOCK, LRU_BLOCK), f32) * LRU_BLOCK ** -0.5
    rec_gate_b = jax.random.normal(ks[13], (N_REC, 2, 2, LRU_WIDTH), f32) * 0.01
    a_target = jax.random.uniform(ks[14], (N_REC, 2, LRU_WIDTH), f32, minval=0.9, maxval=0.999)
    s = a_target ** (1.0 / RG_C)
    rec_lam = jnp.log(s) - jnp.log1p(-s)
    rec_w_out = jax.random.normal(ks[15], (N_REC, LRU_WIDTH, D_MODEL), f32) * LRU_WIDTH ** -0.5
    fou_w_out = jax.random.normal(ks[16], (N_FOU, D_MODEL, D_MODEL), f32) * D_MODEL ** -0.5
    return {'x': x, 'c': c, 'ctx': ctx, 'c_ctx': c_ctx, 'mod_w': mod_w, 'mod_b': mod_b,
            'norm_g': norm_g, 'ffn_w_in': ffn_w_in, 'ffn_w_out': ffn_w_out,
            'rec_w_in': rec_w_in, 'rec_conv_w': rec_conv_w, 'rec_conv_b': rec_conv_b,
            'rec_gate_w': rec_gate_w, 'rec_gate_b': rec_gate_b, 'rec_lam': rec_lam,
            'rec_w_out': rec_w_out, 'fou_w_out': fou_w_out}


def reference(x, c, ctx, c_ctx, mod_w, mod_b, norm_g, ffn_w_in, ffn_w_out, rec_w_in, rec_conv_w,
              rec_conv_b, rec_gate_w, rec_gate_b, rec_lam, rec_w_out, fou_w_out):
    rows = x.shape[1] // GRID_W
    last_rec = ((DEPTH - 1) // N_MIXERS) * N_MIXERS
    for i in range(DEPTH):
        is_rec = (i % N_MIXERS) == 0
        j = i // N_MIXERS
        ctx_used = i <= last_rec
        ctx_full = i < last_rec
        g = norm_g[i]
        sh1, sc1, gt1, sh2, sc2, gt2, sh3, sc3, gt3 = adaln(c, mod_w[i], mod_b[i])
        x = ffn_sublayer(x, g[0], g[1], sh1, sc1, gt1, ffn_w_in[i, 0], ffn_w_out[i, 0])
        hc = None
        if ctx_used:
            m_ctx = adaln(c_ctx[None], mod_w[i], mod_b[i])
            ctx = ffn_sublayer(ctx, g[0], g[1], m_ctx[0], m_ctx[1], m_ctx[2], ffn_w_in[i, 0], ffn_w_out[i, 0])
            hc = modulate(rmsnorm(ctx, g[2]), m_ctx[3], m_ctx[4])
        hx = modulate(rmsnorm(x, g[2]), sh2, sc2)
        if is_rec:
            y, yc = rglru_mixer(hx, hc, rows, rec_w_in[j], rec_conv_w[j], rec_conv_b[j], rec_gate_w[j],
                                rec_gate_b[j], rec_lam[j], rec_w_out[j], ctx_full)
        else:
            y = fourier_mixer(hx, fou_w_out[j])
            yc = fourier_mixer(hc, fou_w_out[j]) if ctx_full else None
        x = x + gt2 * rmsnorm(y, g[3])
        x = ffn_sublayer(x, g[4], g[5], sh3, sc3, gt3, ffn_w_in[i, 1], ffn_w_out[i, 1])
        if ctx_full:
            ctx = ctx + m_ctx[5] * rmsnorm(yc, g[3])
            ctx = ffn_sublayer(ctx, g[4], g[5], m_ctx[6], m_ctx[7], m_ctx[8], ffn_w_in[i, 1], ffn_w_out[i, 1])
    return x
```

```python
import math
from contextlib import ExitStack
import numpy as np
import ml_dtypes
import concourse.bass as bass
import concourse.mybir as mybir
from concourse.bass_utils import run_bass_kernel_spmd

F32 = mybir.dt.float32
BF16 = mybir.dt.bfloat16
AF = mybir.ActivationFunctionType
ALU = mybir.AluOpType

D = 2048
KC = 16
SEQ = 2048
TT = 512
NT = 4
CTX = 256
DFF = 5632
JC = 44
EPS = 1e-6
NV = 688
V_C, V_MODB, V_NG, V_GB, V_LAM, V_CW, V_CB = 0, 32, 320, 512, 576, 608, 672
SB_BASE = 16384 + 128
SB_LIMIT = 228992


class Res:
    __slots__ = ("name", "w", "r")

    def __init__(self, name=""):
        self.name = name
        self.w = None
        self.r = []


class Sched:
    def __init__(self, nc):
        self.nc = nc
        self.es = ExitStack()
        self.eng = {"pe": nc.tensor, "act": nc.scalar, "dve": nc.vector,
                    "pool": nc.gpsimd, "sp": nc.sync}
        self.sems = {}
        self.cnt = {}
        self.seen = {k: {} for k in self.eng}
        for k in ("pe", "act", "dve", "pool"):
            self._sem("E_" + k)
        self.sb_off = SB_BASE
        self.sb_top = SB_LIMIT
        self.uid = 0

    def _sem(self, key):
        if key not in self.sems:
            self.sems[key] = self.es.enter_context(self.nc.semaphore(key))
            self.cnt[key] = 0
        return self.sems[key]

    def sbuf(self, name, shape, dtype):
        esz = 2 if dtype == BF16 else 4
        nbytes = int(np.prod(shape[1:])) * esz
        nbytes = (nbytes + 63) // 64 * 64
        self.uid += 1
        t = self.nc.alloc_sbuf_tensor_at(f"{name}_{self.uid}", list(shape), dtype, offset=self.sb_off)
        self.sb_off += nbytes
        assert self.sb_off <= self.sb_top, (name, self.sb_off, self.sb_top)
        return t

    def sbuf_top(self, name, shape, dtype):
        esz = 2 if dtype == BF16 else 4
        nbytes = int(np.prod(shape[1:])) * esz
        nbytes = (nbytes + 63) // 64 * 64
        self.uid += 1
        self.sb_top -= nbytes
        assert self.sb_off <= self.sb_top, (name, self.sb_off, self.sb_top)
        return self.nc.alloc_sbuf_tensor_at(f"{name}_{self.uid}", list(shape), dtype, offset=self.sb_top)

    def soft_switch(self):
        for s in ("sp", "pool"):
            for k in ("pe", "act", "dve", "pool"):
                key = "E_" + k
                if self.cnt[key] > 0:
                    self._wait(s, (key, self.cnt[key]))

    def mark(self):
        return self.sb_off

    def reset(self, m):
        self.sb_off = m

    def _wait(self, stream, ev):
        if ev is None:
            return
        key, val = ev
        if self.seen[stream].get(key, 0) >= val:
            return
        self.seen[stream][key] = val
        self.eng[stream].wait_ge(self.sems[key], val)

    def _deps(self, stream, reads, writes):
        for r in reads:
            self._wait(stream, r.w)
        for w in writes:
            self._wait(stream, w.w)
            for ev in w.r:
                self._wait(stream, ev)

    def _commit(self, ev, reads, writes):
        for r in reads:
            r.r.append(ev)
        for w in writes:
            w.w = ev
            w.r = []

    def op(self, stream, fn, reads=(), writes=()):
        self._deps(stream, reads, writes)
        ins = fn(self.eng[stream])
        key = "E_" + stream
        self.cnt[key] += 1
        ins.then_inc(self.sems[key], 1)
        ev = (key, self.cnt[key])
        self._commit(ev, reads, writes)
        return ev

    def dma(self, stream, dsem, fn, reads=(), writes=()):
        self._deps(stream, reads, writes)
        key = "D_" + dsem
        sem = self._sem(key)
        inss = fn(self.eng[stream])
        if not isinstance(inss, (list, tuple)):
            inss = [inss]
        for ins in inss:
            ins.then_inc(sem, 16)
            self.cnt[key] += 16
        ev = (key, self.cnt[key])
        self._commit(ev, reads, writes)
        return ev

    def barrier(self):
        for s in self.eng:
            for key, c in self.cnt.items():
                if c > 0:
                    self._wait(s, (key, c))

    def close(self):
        self.es.close()


def rev_ap(ap):
    pat = [list(p) for p in ap.ap]
    assert len(pat) == 2, pat
    step, n = pat[1]
    return bass.AP(ap.tensor, ap.offset + step * (n - 1), [pat[0], [-step, n]])


class Ring:
    def __init__(self, S, name, n, shape, dtype):
        self.name = name
        self.n = n
        self.bufs = [S.sbuf(f"{name}{i}", shape, dtype) for i in range(n)]
        self.res = [Res(f"{name}{i}") for i in range(n)]
        self.i = 0

    def next(self):
        k = self.i % self.n
        self.i += 1
        return self.bufs[k], self.res[k], f"{self.name}{k}"


def mm_group(e, out, pairs):
    n = len(pairs)
    ins = None
    for i, (l, r) in enumerate(pairs):
        ins = e.matmul(out, l, r, start=(i == 0), stop=(i == n - 1))
    return ins


def build(nsub=6):
    nc = bass.Bass("TRN2", target_bir_lowering=False)
    dt_in = lambda name, shape, dt=F32: nc.dram_tensor(name, list(shape), dt, kind="ExternalInput").ap()
    xT = dt_in("xT", [KC, 128, SEQ])
    ctxT = dt_in("ctxT", [KC, 128, CTX])
    vecs_d = dt_in("vecs", [128, NV])
    modw = dt_in("modw", [2, 144, 128, 16 * 128])
    win = dt_in("win", [2, 2, JC, 128, 16 * 256])
    wout = dt_in("wout", [2, 2, KC, 128, JC * 128])
    recin = dt_in("recin", [KC, 128, 16 * 256])
    recout = dt_in("recout", [KC, 128, 16 * 128])
    fouout = dt_in("fouout", [KC, 128, 16 * 128])
    gw = dt_in("gw", [KC, 128, 4 * 128])
    cs_d = dt_in("cs", [128, 2 * 512], BF16)
    dftn = dt_in("dftn", [NT, 2, 128, 16 * 512], BF16)
    outT = nc.dram_tensor("outT", [KC, 128, SEQ], F32, kind="ExternalOutput").ap()
    xs = nc.dram_tensor("xs", [KC, 128, SEQ], F32, kind="Internal").ap()
    ctxs = nc.dram_tensor("ctxs", [KC, 128, CTX], F32, kind="Internal").ap()
    zs = nc.dram_tensor("zs", [KC, 128, SEQ], BF16, kind="Internal").ap()
    winc = nc.dram_tensor("winc", [2, 2, JC, 128, 16 * 256], BF16, kind="Internal").ap()
    woutc = nc.dram_tensor("woutc", [2, 2, KC, 128, JC * 128], BF16, kind="Internal").ap()
    mixc = nc.dram_tensor("mixc", [2, KC, 128, 16 * 128], BF16, kind="Internal").ap()

    S = Sched(nc)
    PA = S.es.enter_context(nc.psum_tensor("PA", [128, 2048], F32))
    PB = S.es.enter_context(nc.psum_tensor("PB", [128, 2048], F32))
    rPA = [Res(f"pa{i}") for i in range(4)]
    rPB = [Res(f"pb{i}") for i in range(4)]
    bank = lambda P, b, n=512: P[:, b * 512:b * 512 + n]

    def dres(name, ntile):
        return [[Res(f"{name}{t}_{m}") for m in range(KC)] for t in range(ntile)]
    r_xT, r_xs, r_out = dres("xT", NT), dres("xs", NT), dres("out", NT)
    r_ctxT, r_ctxs = dres("ctxT", 1), dres("ctxs", 1)
    r_zs = [Res(f"zs{c}") for c in range(KC)]

    vecs = S.sbuf("vecs", [128, NV], F32)
    mods = [S.sbuf(f"mods{l}", [128, 288], F32) for l in range(2)]
    der = S.sbuf("der", [128, 384], F32)
    ones = S.sbuf("ones", [128, 128], BF16)
    epsb = S.sbuf("epsb", [128, 1], F32)
    oneb = S.sbuf("oneb", [128, 1], F32)
    scb = S.sbuf("scb", [128, 32], BF16)
    coef = S.sbuf("coef", [128, 64], F32)
    r_vecs, r_mods, r_der, r_const, r_scb, r_coef = Res(), [Res(), Res()], Res(), Res(), Res(), Res()
    r_modps = [Res(f"modps{i}") for i in range(144)]

    S.dma("sp", "vecs", lambda e: e.dma_start(out=vecs[:], in_=vecs_d), writes=[r_vecs])
    S.op("dve", lambda e: e.memset(ones[:], 1.0), writes=[r_const])
    S.op("dve", lambda e: e.memset(epsb[:], EPS), writes=[r_const])
    S.op("dve", lambda e: e.memset(oneb[:], 1.0), writes=[r_const])

    def vcol(base, idx):
        return vecs[:, base + idx:base + idx + 1]

    def modcol(l, q, kc, which):
        i = (q * 16 + kc) * 2 + which
        return mods[l][:, i:i + 1]

    def modvec(l, q, which):
        t = mods[l]
        a = t[:, 0:1]
        return bass.AP(a.tensor, a.offset + q * 32 + which, [list(a.ap[0]), [2, 16]])

    def dercol(l, s, which, kind, kc):
        i = ((l * 3 + s) * 2 + which) * 32 + kind * 16 + kc
        return der[:, i:i + 1]

    def adaln_load(l, idx, wm):
        buf, rb, dn = wm.next()
        S.dma("pool", dn, lambda e: e.dma_start(out=buf[:], in_=modw[l, idx]), writes=[rb])
        return buf, rb

    def adaln_mm(idx, buf, rb):
        S.op("pe", lambda e: mm_group(e, PB[:, 1536 + idx * 2:1536 + idx * 2 + 2],
                                      [(buf[:, k * 128:(k + 1) * 128], scb[:, 2 * k:2 * k + 2]) for k in range(KC)]),
             reads=[rb, r_scb], writes=[r_modps[idx]])

    def adaln_tasks(l, idxs, wm, lag=2):
        pend = []
        tasks = []

        def mk_load(idx):
            def f():
                pend.append((idx,) + adaln_load(l, idx, wm))
            return f

        def mk_mm():
            def f():
                idx, buf, rb = pend.pop(0)
                adaln_mm(idx, buf, rb)
            return f
        seq = list(idxs)
        for i, idx in enumerate(seq):
            def both(idx=idx, i=i):
                mk_load(idx)()
                if i >= lag:
                    mk_mm()()
            tasks.append(both)
        for _ in range(min(lag, len(seq))):
            tasks.append(mk_mm())
        return tasks

    def adaln_evac(l, a, b):
        n = b - a
        for which in range(2):
            def ev(e, which=which):
                t = mods[l][:, 0:1]
                o = bass.AP(t.tensor, t.offset + 2 * a + which, [list(t.ap[0]), [2, n]])
                p = PB[:, 1536:1537]
                pi = bass.AP(p.tensor, p.offset + 2 * a + which, [list(p.ap[0]), [2, n]])
                return e.tensor_tensor(out=o, in0=pi, in1=vecs[:, V_MODB + l * 144 + a:V_MODB + l * 144 + b],
                                       op=ALU.add)
            S.op("dve", ev, reads=r_modps[a:b] + [r_vecs], writes=[r_mods[l]])

    def der_compute(l, s):
        q0 = 3 * s
        gc = 0.5 if s != 1 else 1.0
        for which in range(2):
            def da(e, which=which):
                i0 = ((l * 3 + s) * 2 + which) * 32
                return e.scalar_tensor_tensor(
                    out=der[:, i0:i0 + 16], in0=modvec(l, q0 + 1, which), scalar=1.0,
                    in1=vecs[:, V_NG + (l * 6 + 2 * s) * 16:V_NG + (l * 6 + 2 * s) * 16 + 16],
                    op0=ALU.add, op1=ALU.mult)
            S.op("dve", da, reads=[r_mods[l], r_vecs], writes=[r_der])

            def dg(e, which=which):
                i0 = ((l * 3 + s) * 2 + which) * 32 + 16
                return e.scalar_tensor_tensor(
                    out=der[:, i0:i0 + 16], in0=modvec(l, q0 + 2, which), scalar=gc,
                    in1=vecs[:, V_NG + (l * 6 + 2 * s + 1) * 16:V_NG + (l * 6 + 2 * s + 1) * 16 + 16],
                    op0=ALU.mult, op1=ALU.mult)
            S.op("dve", dg, reads=[r_mods[l], r_vecs], writes=[r_der])

    def adaln_start():
        S.op("act", lambda e: e.activation(out=scb[:], in_=vecs[:, V_C:V_C + 32], func=AF.Silu),
             reads=[r_vecs], writes=[r_scb])
        m0 = S.mark()
        wm = Ring(S, "wm", 6, [128, 16 * 128], BF16)
        for t in adaln_tasks(0, range(0, 48), wm, lag=4):
            t()
        adaln_evac(0, 0, 48)
        der_compute(0, 0)
        S.barrier()
        S.reset(m0)

    class Work:
        pass

    def alloc_common(W, n_xinA=3):
        W.xy = S.sbuf("xy", [128, KC, TT], F32)
        W.r_xy = [Res(f"xy{m}") for m in range(KC)]
        W.rs = S.sbuf("rs", [128, TT], F32)
        W.r_rs = Res("rs")
        W.rsq = S.sbuf("rsq", [128, TT], F32)
        W.r_rsq = Res("rsq")
        W.tmp = Ring(S, "tmp", 2, [128, TT], F32)
        W.sq2 = Ring(S, "sq2", 2, [128, TT], BF16)
        W.xin = Ring(S, "xin", 3, [128, TT], F32)
        W.xinA = Ring(S, "xinA", n_xinA, [128, TT], F32)
        W.pre_alt = 0
        if n_xinA > 3:
            W.rs2 = S.sbuf("rs2", [128, TT], F32)
            W.r_rs2 = Res("rs2")

    M_BASE = S.mark()
    WC = Work()
    alloc_common(WC)
    P_MARK = S.mark()

    def rstd_from_stat(W, T):
        S.op("act", lambda e: e.activation(out=W.rs[:, :T], in_=bank(PB, 2, T), func=AF.Sqrt,
                                           bias=epsb[:], scale=1.0 / D),
             reads=[rPB[2], r_const], writes=[W.r_rs])
        S.op("dve", lambda e: e.reciprocal(out=W.rs[:, :T], in_=W.rs[:, :T]), reads=[W.r_rs], writes=[W.r_rs])

    def pre_phase(W, src, r_src, t0, T, l, s, which, dst, r_dst):
        S.dma("sp", "xy", lambda e: e.dma_start(out=W.xy[:, :, :T],
                                                in_=src[:, :, t0:t0 + T].rearrange("c p t -> p c t")),
              reads=r_src, writes=W.r_xy)
        for kc in range(KC):
            S.op("act", lambda e, kc=kc: e.activation(out=dst(kc), in_=W.xy[:, kc, :T], func=AF.Square),
                 reads=[W.r_xy[kc]], writes=[r_dst(kc)])
        S.op("pe", lambda e: mm_group(e, bank(PB, 2, T), [(ones[:], dst(kc)) for kc in range(KC)]),
             reads=[r_const] + [r_dst(kc) for kc in range(KC)], writes=[rPB[2]])
        rstd_from_stat(W, T)
        for kc in range(KC):
            tb, rt, _ = W.tmp.next()
            S.op("dve", lambda e, kc=kc, tb=tb: e.tensor_tensor(out=tb[:, :T], in0=W.xy[:, kc, :T],
                                                                 in1=W.rs[:, :T], op=ALU.mult),
                 reads=[W.r_xy[kc], W.r_rs], writes=[rt])
            S.op("act", lambda e, kc=kc, tb=tb: e.activation(
                out=dst(kc), in_=tb[:, :T], func=AF.Identity,
                bias=modcol(l, 3 * s, kc, which), scale=dercol(l, s, which, 0, kc)),
                reads=[rt, r_der, r_mods[l]], writes=[r_dst(kc)])

    def y_phase(W, wtile, nk, rhs, r_rhs, T, wo, cache=None, inter=None):
        pend = None
        inter = list(inter) if inter else []
        n_inter = len(inter)
        for m in range(KC):
            want_left = n_inter - (n_inter * (m + 1)) // KC
            while len(inter) > want_left:
                inter.pop(0)()
            buf, rb, dn = wo.next()
            if cache is not None and not cache[2]:
                S.dma("pool", dn, lambda e, buf=buf, m=m: e.dma_start(out=buf[:, :nk * 128], in_=wtile(m)),
                      reads=[cache[1][m]], writes=[rb])
            else:
                S.dma("pool", dn, lambda e, buf=buf, m=m: e.dma_start(
                    out=buf[:, :nk * 128], in_=wtile(m), max_dma_last_dim=8192), writes=[rb])
                if cache is not None:
                    S.dma("sp", dn + "s", lambda e, buf=buf, m=m: e.dma_start(out=cache[0](m), in_=buf[:, :nk * 128]),
                          reads=[rb], writes=[cache[1][m]])
            pb = m % 2
            S.op("pe", lambda e, buf=buf, pb=pb: mm_group(
                e, bank(PB, pb, T), [(buf[:, k * 128:(k + 1) * 128], rhs(k)) for k in range(nk)]),
                reads=[rb] + r_rhs, writes=[rPB[pb]])
            if pend is not None:
                pend()
            S.op("act", lambda e, m=m, pb=pb: e.activation(out=W.xy[:, m, :T], in_=bank(PB, pb, T), func=AF.Copy),
                 reads=[rPB[pb]], writes=[W.r_xy[m]])
            sb, rsq, _ = W.sq2.next()
            S.op("act", lambda e, sb=sb, pb=pb: e.activation(out=sb[:, :T], in_=bank(PB, pb, T), func=AF.Square),
                 reads=[rPB[pb]], writes=[rsq])

            def stat(m=m, sb=sb, rsq=rsq):
                S.op("pe", lambda e: e.matmul(bank(PB, 2, T), ones[:], sb[:, :T], start=(m == 0), stop=(m == KC - 1)),
                     reads=[rsq, r_const], writes=[rPB[2]])
            pend = stat
        pend()

    def post_steps(W, src, r_src, dst, r_dstd, t0, T, l, s, which):
        loads = {}
        dpp = W.xin.n - 3 if W.xin.n >= 6 else W.xin.n - 1

        def load(m):
            xb, rx, dn = W.xin.next()
            S.dma("sp", dn, lambda e: e.dma_start(out=xb[:, :T], in_=src[m, :, t0:t0 + T]),
                  reads=[r_src[m]], writes=[rx])
            loads[m] = (xb, rx, dn)

        def step_r():
            S.op("act", lambda e: e.activation(out=W.rsq[:, :T], in_=bank(PB, 2, T), func=AF.Sqrt,
                                               bias=epsb[:], scale=1.0 / D),
                 reads=[rPB[2], r_const], writes=[W.r_rsq])
            S.op("dve", lambda e: e.reciprocal(out=W.rsq[:, :T], in_=W.rsq[:, :T]), reads=[W.r_rsq],
                 writes=[W.r_rsq])
            for i in range(dpp):
                load(i)

        def mk(m):
            def f():
                if m + dpp < KC:
                    load(m + dpp)
                xb, rx, dn = loads.pop(m)
                S.op("dve", lambda e: e.tensor_tensor(out=W.xy[:, m, :T], in0=W.xy[:, m, :T], in1=W.rsq[:, :T],
                                                      op=ALU.mult),
                     reads=[W.r_rsq], writes=[W.r_xy[m]])
                S.op("dve", lambda e: e.scalar_tensor_tensor(
                    out=xb[:, :T], in0=W.xy[:, m, :T], scalar=dercol(l, s, which, 1, m), in1=xb[:, :T],
                    op0=ALU.mult, op1=ALU.add), reads=[W.r_xy[m], r_der], writes=[rx])
                S.dma("sp", dn + "s", lambda e: e.dma_start(out=dst[m, :, t0:t0 + T], in_=xb[:, :T]),
                      reads=[rx], writes=[r_dstd[m]])
            return f
        return [step_r] + [mk(m) for m in range(KC)]

    def post_phase(W, src, r_src, dst, r_dstd, t0, T, l, s, which):
        for st in post_steps(W, src, r_src, dst, r_dstd, t0, T, l, s, which):
            st()

    def pre_steps(W, src, r_src, t0, T, l, s, which, dst, r_dst, alt=0):
        loads = {}
        dp = 3 if W.xinA.n >= 10 else 2
        sb_ = alt
        rs_, r_rs_ = (W.rs, W.r_rs) if alt == 0 else (W.rs2, W.r_rs2)

        def load(tag, kc):
            xb, rx, dn = W.xinA.next()
            S.dma("sp", dn, lambda e: e.dma_start(out=xb[:, :T], in_=src[kc, :, t0:t0 + T]),
                  reads=[r_src[kc]], writes=[rx])
            loads[(tag, kc)] = (xb, rx)

        def mkA(kc):
            def f():
                if kc == 0:
                    for i in range(dp):
                        load("A", i)
                if kc + dp < KC:
                    load("A", kc + dp)
                xb, rx = loads.pop(("A", kc))
                S.op("act", lambda e: e.activation(out=dst(kc), in_=xb[:, :T], func=AF.Square),
                     reads=[rx], writes=[r_dst(kc)])
                S.op("pe", lambda e: e.matmul(bank(PA, sb_, T), ones[:], dst(kc), start=(kc == 0), stop=(kc == KC - 1)),
                     reads=[r_const, r_dst(kc)], writes=[rPA[sb_]])
            return f

        def step_r():
            for i in range(dp):
                load("B", i)
            S.op("act", lambda e: e.activation(out=rs_[:, :T], in_=bank(PA, sb_, T), func=AF.Sqrt,
                                               bias=epsb[:], scale=1.0 / D),
                 reads=[rPA[sb_], r_const], writes=[r_rs_])
            S.op("dve", lambda e: e.reciprocal(out=rs_[:, :T], in_=rs_[:, :T]), reads=[r_rs_], writes=[r_rs_])

        def mkB(kc):
            def f():
                if kc + dp < KC:
                    load("B", kc + dp)
                xb, rx = loads.pop(("B", kc))
                S.op("dve", lambda e: e.tensor_tensor(out=xb[:, :T], in0=xb[:, :T], in1=rs_[:, :T], op=ALU.mult),
                     reads=[r_rs_], writes=[rx])
                S.op("act", lambda e: e.activation(
                    out=dst(kc), in_=xb[:, :T], func=AF.Identity,
                    bias=modcol(l, 3 * s, kc, which), scale=dercol(l, s, which, 0, kc)),
                    reads=[rx, r_der, r_mods[l]], writes=[r_dst(kc)])
            return f
        return [mkA(kc) for kc in range(KC)] + [step_r] + [mkB(kc) for kc in range(KC)]

    def pre_pipeline(W, jobs_args, carry=None):
        lists = [pre_steps(W, *a, alt=i % 2) for i, a in enumerate(jobs_args)]
        carry = list(carry) if carry else []
        for st in lists[0][:KC]:
            st()
            if carry:
                carry.pop(0)()
        for i, L in enumerate(lists):
            L[KC]()
            nxt = lists[i + 1][:KC] if i + 1 < len(lists) else []
            for k in range(KC):
                L[KC + 1 + k]()
                if carry:
                    carry.pop(0)()
                if nxt:
                    nxt[k]()
        for st in carry:
            st()

    cache_res = {}

    def get_cache_res(l, f):
        if (l, f) not in cache_res:
            cache_res[(l, f)] = dict(win=[Res(f"winc{l}{f}_{j}") for j in range(JC)],
                                     wout=[Res(f"woutc{l}{f}_{m}") for m in range(KC)], pre=False)
        return cache_res[(l, f)]

    pc_slots = [Res(f"pcslot{i}") for i in range(8)]
    pc_i = [0]

    def preconv_tasks(l, f):
        cr = get_cache_res(l, f)
        cr["pre"] = True
        tasks = []

        def mk(src, dstc, r):
            def t():
                k = pc_i[0] % len(pc_slots)
                pc_i[0] += 1
                S.dma("pool", f"pc{k}", lambda e: e.dma_start(out=dstc, in_=src, max_dma_last_dim=8192),
                      writes=[r, pc_slots[k]])
            return t
        for j in range(JC):
            tasks.append(mk(win[l, f, j], winc[l, f, j], cr["win"][j]))
        for m in range(KC):
            tasks.append(mk(wout[l, f, m], woutc[l, f, m], cr["wout"][m]))
        return tasks

    def ffn(l, f, jobs, bg_l=None, bg_idxs=(), carry=None, defer=False):
        s = 0 if f == 0 else 2
        S.soft_switch()
        m0 = S.mark()
        W = WC
        wi = Ring(S, "wi", 5, [128, 16 * 256], BF16)
        wo = Ring(S, "wo", 3, [128, JC * 128], BF16)
        wm = Ring(S, "wm", 3, [128, 16 * 128], BF16)
        bg = adaln_tasks(bg_l, bg_idxs, wm, lag=2) if bg_l is not None else []
        n_bg_iters = max(1, (len(jobs) - 1) * JC)
        bg_done = 0
        bg_iter = 0
        cr = get_cache_res(l, f)
        r_winc, r_woutc, pre_conv = cr["win"], cr["wout"], cr["pre"]
        hb = S.sbuf("hb", [128, KC, TT], BF16)
        r_hb = [Res(f"hb{k}") for k in range(KC)]
        hid = S.sbuf("hid", [128, JC, TT], BF16)
        r_hid = [Res(f"hid{j}") for j in range(JC)]
        def mk_pre(job):
            (src, r_src, dst, r_dd, t0, T, ti, which) = job
            return pre_steps(W, src, r_src[ti], t0, T, l, s, which,
                             lambda kc, T=T: hb[:, kc, :T], lambda kc: r_hb[kc])

        def mk_post(job):
            (src, r_src, dst, r_dd, t0, T, ti, which) = job
            return post_steps(W, src, r_src[ti], dst, r_dd[ti], t0, T, l, s, which)
        for st in mk_pre(jobs[0]):
            st()
        posts = list(carry) if carry else []
        for ji, (src, r_src, dst, r_dd, t0, T, ti, which) in enumerate(jobs):
            n_post = len(posts)
            for j in range(JC):
                want_left = n_post - (n_post * (j + 1)) // JC
                while len(posts) > want_left:
                    posts.pop(0)()
                buf, rb, dn = wi.next()
                if ji == 0 and not pre_conv:
                    S.dma("pool", dn, lambda e, buf=buf, j=j: e.dma_start(
                        out=buf[:], in_=win[l, f, j], max_dma_last_dim=8192), writes=[rb])
                    S.dma("sp", dn + "s", lambda e, buf=buf, j=j: e.dma_start(out=winc[l, f, j], in_=buf[:]),
                          reads=[rb], writes=[r_winc[j]])
                else:
                    S.dma("pool", dn, lambda e, buf=buf, j=j: e.dma_start(out=buf[:], in_=winc[l, f, j]),
                          reads=[r_winc[j]], writes=[rb])
                    if ji > 0:
                        bg_iter += 1
                        want = (len(bg) + bg_done) * bg_iter // n_bg_iters
                        while bg and bg_done < want:
                            bg.pop(0)()
                            bg_done += 1
                pb = j % 2

                def pe(e, buf=buf, pb=pb):
                    mm_group(e, bank(PA, pb, T), [(buf[:, k * 256:k * 256 + 128], hb[:, k, :T]) for k in range(KC)])
                    return mm_group(e, bank(PA, 2 + pb, T),
                                    [(buf[:, k * 256 + 128:k * 256 + 256], hb[:, k, :T]) for k in range(KC)])
                S.op("pe", pe, reads=[rb] + r_hb, writes=[rPA[pb], rPA[2 + pb]])
                tb, rt, _ = W.tmp.next()
                S.op("act", lambda e, tb=tb, pb=pb: e.activation(out=tb[:, :T], in_=bank(PA, pb, T), func=AF.Silu),
                     reads=[rPA[pb]], writes=[rt])
                S.op("dve", lambda e, tb=tb, pb=pb, j=j: e.tensor_tensor(
                    out=hid[:, j, :T], in0=tb[:, :T], in1=bank(PA, 2 + pb, T), op=ALU.mult),
                    reads=[rt, rPA[2 + pb]], writes=[r_hid[j]])
            inter = mk_pre(jobs[ji + 1]) if ji + 1 < len(jobs) else None
            if ji == 0 and not pre_conv:
                y_phase(W, lambda m: wout[l, f, m], JC, lambda k: hid[:, k, :T], r_hid, T, wo,
                        cache=(lambda m: woutc[l, f, m], r_woutc, True), inter=inter)
            else:
                y_phase(W, lambda m: woutc[l, f, m], JC, lambda k: hid[:, k, :T], r_hid, T, wo,
                        cache=(None, r_woutc, False), inter=inter)
            posts = mk_post(jobs[ji])
        while bg:
            bg.pop(0)()
        S.reset(m0)
        if defer:
            return posts
        for st in posts:
            st()
        S.barrier()
        return None

    def mixer_out(l, wsrc, from_zs, hx=None, r_hx=None, dst=None, r_dst=None):
        W = WC
        W2 = Work()
        W2.__dict__.update(W.__dict__)
        W2.xy = S.sbuf("xyb", [128, KC, TT], F32)
        W2.r_xy = [Res(f"xyb{m}") for m in range(KC)]
        W2.rsq = S.sbuf("rsqb", [128, TT], F32)
        W2.r_rsq = Res("rsqb")
        W2.xin = Ring(S, "xinb", 8, [128, TT], F32)
        W3 = Work()
        W3.__dict__.update(W.__dict__)
        W3.xin = W2.xin
        Ws = [W2, W3, W2, W]
        wo = Ring(S, "wo", 3, [128, KC * 128], BF16)
        r_mixc = [Res(f"mixc{m}") for m in range(KC)]
        if from_zs:
            hbr = Ring(S, "hbz", 2, [128, KC, TT], BF16)
        posts = None
        hbq = {}

        def load_hbz(t):
            if from_zs and t < NT:
                hb, r_hb1, dn = hbr.next()
                S.dma("sp", dn, lambda e: e.dma_start(
                    out=hb[:], in_=zs[:, :, t * TT:(t + 1) * TT].rearrange("c p t -> p c t")),
                    reads=r_zs, writes=[r_hb1])
                hbq[t] = (hb, r_hb1)
        load_hbz(0)
        for t in range(NT):
            t0 = t * TT
            Wt = Ws[t]
            load_hbz(t + 1)
            if from_zs:
                hb, r_hb1 = hbq.pop(t)
                rhs = lambda k, hb=hb: hb[:, k, :]
                rr = [r_hb1]
            else:
                rhs = lambda k, t0=t0: hx[:, k, t0:t0 + TT]
                rr = r_hx
            if t == 0:
                y_phase(Wt, lambda m: wsrc[m], KC, rhs, rr, TT, wo, inter=posts,
                        cache=(lambda m: mixc[l, m], r_mixc, True))
            else:
                y_phase(Wt, lambda m: mixc[l, m], KC, rhs, rr, TT, wo, inter=posts, cache=(None, r_mixc, False))
            posts = post_steps(Wt, xs, r_xs[t], dst, r_dst[t], t0, TT, l, 1, 0)
        return posts

    def mixer_pre_work():
        W = Work()
        W.__dict__.update(WC.__dict__)
        W.xinA = Ring(S, "xinB", 10, [128, TT], F32)
        W.rs2 = S.sbuf("rs2", [128, TT], F32)
        W.r_rs2 = Res("rs2")
        return W

    def rglru(carry=None):
        l = 0
        S.soft_switch()
        m0 = S.mark()
        top0 = S.sb_top
        hx = S.sbuf_top("hx", [128, KC, SEQ], BF16)
        r_hx = [Res(f"hx{k}") for k in range(KC)]
        hcb = S.sbuf_top("hcb", [128, KC, CTX], BF16)
        r_hcb = [Res(f"hcb{k}") for k in range(KC)]
        W = mixer_pre_work()
        r_hxt = [[Res(f"hx{t}_{k}") for k in range(KC)] for t in range(NT)]
        pre_pipeline(W, [(xs, r_xs[t], t * TT, TT, l, 1, 0,
                          (lambda kc, t=t: hx[:, kc, t * TT:(t + 1) * TT]), (lambda kc, t=t: r_hxt[t][kc]))
                         for t in range(NT)] +
                     [(ctxs, r_ctxs[0], 0, CTX, l, 1, 1, (lambda kc: hcb[:, kc, :]), (lambda kc: r_hcb[kc]))],
                     carry=carry)
        S.barrier()
        S.reset(M_BASE)
        wi = Ring(S, "wi", 2, [128, 16 * 256], BF16)
        gwr = Ring(S, "gwr", 3, [128, 4 * 128], BF16)
        vp = S.sbuf("vp", [128, 32, 67], F32)
        vc = S.sbuf("vc", [128, SEQ], F32)
        vcb = S.sbuf("vcb", [128, SEQ], BF16)
        bA = S.sbuf("bA", [128, SEQ], F32)
        bB = [S.sbuf(f"bB{d}", [128, SEQ], F32) for d in range(2)]
        bC = [S.sbuf(f"bC{d}", [128, SEQ], F32) for d in range(2)]
        bH = [S.sbuf(f"bH{d}", [128, SEQ], F32) for d in range(2)]
        bG = [S.sbuf(f"bG{i}", [128, SEQ], F32) for i in range(2)]
        zb = Ring(S, "zb", 1, [128, SEQ], BF16)
        vpc = [S.sbuf(f"vpc{i}", [128, CTX + 3], F32) for i in range(2)]
        vcc = [S.sbuf(f"vcc{i}", [128, CTX], F32) for i in range(2)]
        vccb = [S.sbuf(f"vccb{i}", [128, CTX], BF16) for i in range(2)]
        cA = S.sbuf("cA", [128, CTX], F32)
        cB = [S.sbuf(f"cB{d}", [128, CTX], F32) for d in range(2)]
        cC = [S.sbuf(f"cC{d}", [128, CTX], F32) for d in range(2)]
        cH = S.sbuf("cH", [128, 2, CTX], F32)
        hgb = S.sbuf("hgb", [128, 64], F32)
        r_vp, r_vc, r_vcb = Res(), Res(), Res()
        r_G = [Res(), Res()]
        HALF = SEQ // 2
        r_A = [Res(), Res()]
        r_B = [[Res(), Res()] for d in range(2)]
        r_C = [Res(), Res()]
        r_H = [Res(), Res()]
        r_vpc, r_vcc, r_vccb = [Res(), Res()], [Res(), Res()], [Res(), Res()]
        r_cA, r_cB, r_cC, r_cH = Res(), [Res(), Res()], [Res(), Res()], [Res(), Res()]
        S.op("dve", lambda e: e.memset(vp[:], 0.0), writes=[r_vp])
        for i in range(2):
            S.op("dve", lambda e, i=i: e.memset(vpc[i][:], 0.0), writes=[r_vpc[i]])
        S.op("act", lambda e: e.activation(out=coef[:, 0:32], in_=vecs[:, V_LAM:V_LAM + 32], func=AF.Exp, scale=-1.0),
             reads=[r_vecs], writes=[r_coef])
        S.op("act", lambda e: e.activation(out=coef[:, 0:32], in_=coef[:, 0:32], func=AF.Ln, bias=oneb[:], scale=1.0),
             reads=[r_coef, r_const], writes=[r_coef])
        S.op("dve", lambda e: e.tensor_scalar(out=coef[:, 32:64], in0=coef[:, 0:32], scalar1=-4.0, scalar2=None,
                                              op0=ALU.mult), reads=[r_coef], writes=[r_coef])
        S.op("dve", lambda e: e.tensor_scalar(out=coef[:, 0:32], in0=coef[:, 0:32], scalar1=-8.0, scalar2=None,
                                              op0=ALU.mult), reads=[r_coef], writes=[r_coef])
        S.op("dve", lambda e: e.tensor_scalar(out=hgb[:], in0=vecs[:, V_GB:V_GB + 64], scalar1=0.5, scalar2=None,
                                              op0=ALU.mult), reads=[r_vecs], writes=[r_coef])
        pv3 = PA[:, :].rearrange("p (r w) -> p r w", w=64)
        vc3 = vc[:, :].rearrange("p (r w) -> p r w", w=64)

        def tanh_pair(d, c, ps_r, rps_r, ps_i, rps_i, A, rA, B, rB):
            S.op("act", lambda e: e.activation(out=A, in_=ps_r, func=AF.Tanh, scale=0.5,
                                               bias=hgb[:, (d * 2 + 0) * 16 + c:(d * 2 + 0) * 16 + c + 1]),
                 reads=rps_r + [r_coef], writes=rA)
            S.op("act", lambda e: e.activation(out=B, in_=ps_i, func=AF.Tanh, scale=0.5,
                                               bias=hgb[:, (d * 2 + 1) * 16 + c:(d * 2 + 1) * 16 + c + 1]),
                 reads=rps_i + [r_coef], writes=rB)

        wslots = {}

        def load_w(c):
            if c >= KC:
                return
            buf, rb, dn = wi.next()
            S.dma("pool", dn, lambda e: e.dma_start(out=buf[:], in_=recin[c], max_dma_last_dim=8192), writes=[rb])
            gb, rgb, gdn = gwr.next()
            S.dma("pool", gdn, lambda e: e.dma_start(out=gb[:], in_=gw[c]), writes=[rgb])
            wslots[c] = (buf, rb, gb, rgb)

        def emit_ctxv(c):
            if c >= KC:
                return
            buf, rb, gb, rgb = wslots[c]
            p = c % 2
            S.op("pe", lambda e: mm_group(e, bank(PA, 0, CTX), [(buf[:, k * 256 + 128:k * 256 + 256], hcb[:, k, :])
                                                                for k in range(KC)]),
                 reads=[rb] + r_hcb, writes=[rPA[0]])
            S.op("act", lambda e: e.activation(out=vpc[p][:, 2:2 + CTX], in_=bank(PA, 0, CTX), func=AF.Copy),
                 reads=[rPA[0]], writes=[r_vpc[p]])

        def emit_v(c):
            if c >= KC:
                return
            buf, rb, gb, rgb = wslots[c]
            for tt in range(NT):
                S.op("pe", lambda e, tt=tt: mm_group(e, bank(PA, tt), [
                    (buf[:, k * 256 + 128:k * 256 + 256], hx[:, k, tt * TT:(tt + 1) * TT]) for k in range(KC)]),
                    reads=[rb] + r_hx, writes=[rPA[tt]])

        def stage1(c):
            if c >= KC:
                return
            buf, rb, gb, rgb = wslots[c]
            p = c % 2
            S.op("act", lambda e: e.activation(out=vp[:, :, 2:66], in_=pv3, func=AF.Copy), reads=rPA, writes=[r_vp])
            for tt in range(NT):
                S.op("pe", lambda e, tt=tt: mm_group(e, bank(PA, tt), [
                    (buf[:, k * 256:k * 256 + 128], hx[:, k, tt * TT:(tt + 1) * TT]) for k in range(KC)]),
                    reads=[rb] + r_hx, writes=[rPA[tt]])
            load_w(c + 2)
            S.op("act", lambda e: e.activation(out=bG[p][:], in_=PA[:, :], func=AF.Gelu_apprx_tanh),
                 reads=rPA, writes=[r_G[p]])
            cw = lambda k: vcol(V_CW, k * 16 + c)
            S.op("dve", lambda e: e.tensor_scalar(out=vcc[p][:], in0=vpc[p][:, 2:2 + CTX], scalar1=cw(2),
                                                  scalar2=vcol(V_CB, c), op0=ALU.mult, op1=ALU.add),
                 reads=[r_vpc[p], r_vecs], writes=[r_vcc[p]])
            for k, o in ((0, 0), (1, 1), (3, 3)):
                S.op("dve", lambda e, k=k, o=o: e.scalar_tensor_tensor(
                    out=vcc[p][:], in0=vpc[p][:, o:o + CTX], scalar=cw(k), in1=vcc[p][:], op0=ALU.mult, op1=ALU.add),
                    reads=[r_vpc[p], r_vcc[p], r_vecs], writes=[r_vcc[p]])
            S.op("dve", lambda e: e.tensor_copy(out=vccb[p][:], in_=vcc[p][:]), reads=[r_vcc[p]], writes=[r_vccb[p]])
            S.op("dve", lambda e: e.tensor_scalar(out=vc3, in0=vp[:, :, 2:66], scalar1=cw(2),
                                                  scalar2=vcol(V_CB, c), op0=ALU.mult, op1=ALU.add),
                 reads=[r_vp, r_vecs], writes=[r_vc])
            for k, o in ((0, 0), (1, 1), (3, 3)):
                S.op("dve", lambda e, k=k, o=o: e.scalar_tensor_tensor(
                    out=vc3, in0=vp[:, :, o:o + 64], scalar=cw(k), in1=vc3, op0=ALU.mult, op1=ALU.add),
                    reads=[r_vp, r_vc, r_vecs], writes=[r_vc])
            S.op("dve", lambda e: e.tensor_copy(out=vcb[:], in_=vc[:]), reads=[r_vc], writes=[r_vcb])
            emit_ctxv(c + 1)

        load_w(0)
        load_w(1)
        emit_ctxv(0)
        emit_v(0)
        stage1(0)
        pct = preconv_tasks(0, 1) if nsub >= 3 else []
        n_pct = len(pct)
        for c in range(KC):
            buf, rb, gb, rgb = wslots[c]
            p = c % 2
            while len(pct) > n_pct - (n_pct * (c + 1)) // KC:
                pct.pop(0)()
            gcol = lambda d, g: hgb[:, (d * 2 + g) * 16 + c:(d * 2 + g) * 16 + c + 1]
            chc = lambda d: coef[:, 32 + d * 16 + c:32 + d * 16 + c + 1]
            def pe_c(e):
                ins = None
                for d in range(2):
                    e.matmul(PB[:, d * 512:d * 512 + CTX], gb[:, (d * 2) * 128:(d * 2 + 1) * 128], vccb[p][:],
                             start=True, stop=True)
                    ins = e.matmul(PB[:, d * 512 + CTX:d * 512 + 2 * CTX], gb[:, (d * 2 + 1) * 128:(d * 2 + 2) * 128],
                                   vccb[p][:], start=True, stop=True)
                return ins
            S.op("pe", pe_c, reads=[rgb, r_vccb[p]], writes=[rPB[0], rPB[1]])
            for d in range(2):
                S.op("act", lambda e, d=d: e.activation(out=cA[:], in_=PB[:, d * 512:d * 512 + CTX], func=AF.Tanh,
                                                        scale=0.5, bias=gcol(d, 0)),
                     reads=[rPB[d], r_coef], writes=[r_cA])
                S.op("act", lambda e, d=d: e.activation(out=cB[d][:], in_=PB[:, d * 512 + CTX:d * 512 + 2 * CTX],
                                                        func=AF.Tanh, scale=0.5, bias=gcol(d, 1)),
                     reads=[rPB[d], r_coef], writes=[r_cB[d]])
                S.op("act", lambda e, d=d: e.activation(out=cC[d][:], in_=cA[:], func=AF.Exp, scale=chc(d), bias=chc(d)),
                     reads=[r_cA, r_coef], writes=[r_cC[d]])
            for d in range(2):
                gwa = gb[:, (d * 2 + 0) * 128:(d * 2 + 1) * 128]
                gwx = gb[:, (d * 2 + 1) * 128:(d * 2 + 2) * 128]
                for hf in range(2):
                    lo, hi = hf * HALF, (hf + 1) * HALF

                    def pe_l(e, gwa=gwa, gwx=gwx, lo=lo):
                        for t2 in range(2):
                            e.matmul(bank(PB, t2), gwa, vcb[:, lo + t2 * TT:lo + (t2 + 1) * TT], start=True, stop=True)
                        ins = None
                        for t2 in range(2):
                            ins = e.matmul(bank(PB, 2 + t2), gwx, vcb[:, lo + t2 * TT:lo + (t2 + 1) * TT],
                                           start=True, stop=True)
                        return ins
                    S.op("pe", pe_l, reads=[rgb, r_vcb], writes=rPB)
                    S.op("act", lambda e, d=d, lo=lo, hi=hi: e.activation(
                        out=bA[:, lo:hi], in_=PB[:, 0:HALF], func=AF.Tanh, scale=0.5, bias=gcol(d, 0)),
                        reads=rPB[0:2] + [r_coef], writes=[r_A[hf]])
                    S.op("act", lambda e, d=d, lo=lo, hi=hi: e.activation(
                        out=bB[d][:, lo:hi], in_=PB[:, HALF:SEQ], func=AF.Tanh, scale=0.5, bias=gcol(d, 1)),
                        reads=rPB[2:4] + [r_coef], writes=[r_B[d][hf]])
                S.op("act", lambda e, d=d: e.activation(out=bC[d][:], in_=bA[:], func=AF.Exp, scale=chc(d), bias=chc(d)),
                     reads=r_A + [r_coef], writes=[r_C[d]])
            emit_v(c + 1)
            for d in range(2):
                S.op("dve", lambda e, d=d: e.tensor_tensor(out=cH[:, d, :], in0=cC[d][:], in1=cC[d][:], op=ALU.mult),
                     reads=[r_cC[d]], writes=[r_cH[d]])
                S.op("dve", lambda e, d=d: e.tensor_tensor(out=bH[d][:], in0=bC[d][:], in1=bC[d][:], op=ALU.mult),
                     reads=[r_C[d]], writes=[r_H[d]])
            for d in range(2):
                S.op("act", lambda e, d=d: e.activation(out=cH[:, d, :], in_=cH[:, d, :], func=AF.Sqrt, bias=oneb[:],
                                                        scale=-1.0), reads=[r_cH[d], r_const], writes=[r_cH[d]])
                S.op("act", lambda e, d=d: e.activation(out=bH[d][:], in_=bH[d][:], func=AF.Sqrt, bias=oneb[:],
                                                        scale=-1.0), reads=[r_H[d], r_const], writes=[r_H[d]])
            for d in range(2):
                S.op("dve", lambda e, d=d: e.scalar_tensor_tensor(out=cB[d][:], in0=cB[d][:], scalar=1.0, in1=cH[:, d, :],
                                                                  op0=ALU.add, op1=ALU.mult),
                     reads=[r_cH[d], r_cB[d]], writes=[r_cB[d]])
                S.op("dve", lambda e, d=d: e.scalar_tensor_tensor(out=cB[d][:], in0=cB[d][:], scalar=0.5, in1=vcc[p][:],
                                                                  op0=ALU.mult, op1=ALU.mult),
                     reads=[r_cB[d], r_vcc[p]], writes=[r_cB[d]])
                S.op("dve", lambda e, d=d: e.scalar_tensor_tensor(out=bB[d][:], in0=bB[d][:], scalar=1.0, in1=bH[d][:],
                                                                  op0=ALU.add, op1=ALU.mult),
                     reads=[r_H[d]] + r_B[d], writes=r_B[d])
                S.op("dve", lambda e, d=d: e.scalar_tensor_tensor(out=bB[d][:], in0=bB[d][:], scalar=0.5, in1=vc[:],
                                                                  op0=ALU.mult, op1=ALU.mult),
                     reads=r_B[d] + [r_vc], writes=r_B[d])
            stage1(c + 1)
            S.op("dve", lambda e: e.tensor_tensor_scan(out=cH[:, 0, :], data0=cC[0][:], data1=cB[0][:],
                                                       initial=0.0, op0=ALU.mult, op1=ALU.add),
                 reads=[r_cC[0], r_cB[0]], writes=[r_cH[0]])
            S.op("dve", lambda e: e.tensor_tensor_scan(
                out=bH[0][:], data0=bC[0][:], data1=bB[0][:], initial=cH[:, 0, CTX - 1:CTX], op0=ALU.mult, op1=ALU.add),
                reads=[r_C[0]] + r_B[0] + [r_cH[0]], writes=[r_H[0]])
            S.op("dve", lambda e: e.tensor_tensor_scan(out=rev_ap(cH[:, 1, :]), data0=rev_ap(cC[1][:]),
                                                       data1=rev_ap(cB[1][:]), initial=0.0,
                                                       op0=ALU.mult, op1=ALU.add),
                 reads=[r_cC[1], r_cB[1]], writes=[r_cH[1]])
            S.op("dve", lambda e: e.tensor_tensor_scan(
                out=rev_ap(bH[1][:]), data0=rev_ap(bC[1][:]), data1=rev_ap(bB[1][:]), initial=cH[:, 1, 0:1],
                op0=ALU.mult, op1=ALU.add), reads=[r_C[1]] + r_B[1] + [r_cH[1]], writes=[r_H[1]])
            S.op("dve", lambda e: e.tensor_tensor(out=bH[0][:], in0=bH[0][:], in1=bH[1][:], op=ALU.add),
                 reads=r_H, writes=[r_H[0]])
            zt, rz, zdn = zb.next()
            S.op("dve", lambda e, zt=zt: e.tensor_tensor(out=zt[:], in0=bH[0][:], in1=bG[p][:], op=ALU.mult),
                 reads=[r_H[0], r_G[p]], writes=[rz])
            S.dma("sp", zdn, lambda e, zt=zt, c=c: e.dma_start(out=zs[c], in_=zt[:]), reads=[rz], writes=[r_zs[c]])
        S.barrier()
        S.reset(m0)
        S.sb_top = top0
        posts = mixer_out(0, recout, True, dst=xs, r_dst=r_xs)
        S.reset(m0)
        return posts

    def fourier(carry=None):
        l = 1
        S.soft_switch()
        m0 = S.mark()
        top0 = S.sb_top
        hx = S.sbuf_top("hxf", [128, KC, SEQ], BF16)
        r_hx = [Res(f"hxf{k}") for k in range(KC)]
        W = mixer_pre_work()
        r_hxt = [[Res(f"hxf{t}_{k}") for k in range(KC)] for t in range(NT)]
        pre_pipeline(W, [(xs, r_xs[t], t * TT, TT, l, 1, 0,
                          (lambda kc, t=t: hx[:, kc, t * TT:(t + 1) * TT]), (lambda kc, t=t: r_hxt[t][kc]))
                         for t in range(NT)], carry=carry)
        S.barrier()
        S.reset(M_BASE)
        cs = S.sbuf("cs", [128, 2 * 512], BF16)
        r_cs = Res()
        S.dma("sp", "cs", lambda e: e.dma_start(out=cs[:], in_=cs_d), writes=[r_cs])
        xcs = Ring(S, "xcs", 4, [128, KC, 512], BF16)
        dbuf = Ring(S, "dbuf", 4, [128, 16 * 512], BF16)
        pct = preconv_tasks(1, 1) if nsub >= 6 else []
        n_pct = len(pct)
        for gp in range(4):
            xbs = []
            for g in (2 * gp, 2 * gp + 1):
                xb, rxb, _ = xcs.next()
                xbs.append((g, xb, rxb))
                for n in range(KC):
                    pb = n % 2
                    S.op("pe", lambda e, n=n, pb=pb, g=g: mm_group(e, bank(PA, pb), [
                        (hx[:, 2 * g + jj, n * 128:(n + 1) * 128], cs[:, jj * 512:(jj + 1) * 512])
                        for jj in range(2)]),
                        reads=[r_hx[2 * g], r_hx[2 * g + 1], r_cs], writes=[rPA[pb]])
                    S.op("act" if n % 2 == 0 else "dve",
                         (lambda e, n=n, pb=pb, xb=xb: e.activation(out=xb[:, n, :], in_=bank(PA, pb), func=AF.Copy))
                         if n % 2 == 0 else
                         (lambda e, n=n, pb=pb, xb=xb: e.tensor_copy(out=xb[:, n, :], in_=bank(PA, pb))),
                         reads=[rPA[pb]], writes=[rxb])
            for kt in range(NT):
                dc, rdc, dnc = dbuf.next()
                S.dma("pool", dnc, lambda e, dc=dc, kt=kt: e.dma_start(out=dc[:], in_=dftn[kt, 0]), writes=[rdc])
                ds_, rds, dns = dbuf.next()
                S.dma("pool", dns, lambda e, ds_=ds_, kt=kt: e.dma_start(out=ds_[:], in_=dftn[kt, 1]), writes=[rds])
                while len(pct) > n_pct - (n_pct * (gp * NT + kt + 1)) // (4 * NT):
                    pct.pop(0)()
                i2 = 0
                for (g, xb, rxb) in xbs:
                    for ff in range(2):
                        pb = 2 + i2 % 2
                        i2 += 1
                        S.op("pe", lambda e, ff=ff, pb=pb, xb=xb, dc=dc, ds_=ds_: mm_group(
                            e, bank(PA, pb),
                            [(xb[:, n, ff * 128:(ff + 1) * 128], dc[:, n * 512:(n + 1) * 512]) for n in range(KC)] +
                            [(xb[:, n, 256 + ff * 128:256 + (ff + 1) * 128], ds_[:, n * 512:(n + 1) * 512])
                             for n in range(KC)]),
                            reads=[rxb, rdc, rds], writes=[rPA[pb]])
                        S.op("act", lambda e, ff=ff, pb=pb, kt=kt, g=g: e.activation(
                            out=hx[:, 2 * g + ff, kt * TT:(kt + 1) * TT], in_=bank(PA, pb), func=AF.Copy),
                            reads=[rPA[pb]], writes=[r_hx[2 * g + ff]])
        S.barrier()
        S.reset(m0)
        posts = mixer_out(1, fouout, False, hx=hx, r_hx=r_hx, dst=xs, r_dst=r_xs)
        S.reset(m0)
        S.sb_top = top0
        return posts

    def xjobs(src, r_src, dst, r_dst):
        return [(src, r_src, dst, r_dst, t * TT, TT, t, 0) for t in range(NT)]

    adaln_start()
    final = lambda k: (outT, r_out) if nsub == k else (xs, r_xs)
    d_, rd_ = final(1)
    carry = ffn(0, 0, xjobs(xT, r_xT, d_, rd_) + [(ctxT, r_ctxT, ctxs, r_ctxs, 0, CTX, 0, 1)],
                bg_l=0, bg_idxs=range(48, 144), defer=True)
    adaln_evac(0, 48, 144)
    der_compute(0, 1)
    der_compute(0, 2)
    if nsub >= 2:
        carry = rglru(carry)
    if nsub >= 3:
        d_, rd_ = final(3)
        carry = ffn(0, 1, xjobs(xs, r_xs, d_, rd_), bg_l=1, bg_idxs=range(0, 144), carry=carry, defer=True)
        adaln_evac(1, 0, 144)
        for s_ in range(3):
            der_compute(1, s_)
    if nsub >= 4:
        d_, rd_ = final(4)
        carry = ffn(1, 0, xjobs(xs, r_xs, d_, rd_), carry=carry, defer=True)
    if nsub >= 5:
        carry = fourier(carry)
    if nsub >= 6:
        d_, rd_ = final(6)
        carry = ffn(1, 1, xjobs(xs, r_xs, d_, rd_), carry=carry, defer=True)
    for st in carry:
        st()
    S.barrier()
    if nsub in (2, 5):
        m0 = S.mark()
        cp = Ring(S, "cp", 2, [128, SEQ], F32)
        for m in range(KC):
            b, rb, dn = cp.next()
            S.dma("sp", dn, lambda e, b=b, m=m: e.dma_start(out=b[:], in_=xs[m]),
                  reads=[r_xs[t][m] for t in range(NT)], writes=[rb])
            S.dma("sp", dn + "o", lambda e, b=b, m=m: e.dma_start(out=outT[m], in_=b[:]), reads=[rb],
                  writes=[r_out[t][m] for t in range(NT)])
        S.reset(m0)
    S.barrier()
    S.close()
    return nc


def _fm(v):
    v = np.asarray(v, np.float32)
    lead = v.shape[:-1]
    return np.moveaxis(v.reshape(lead + (KC, 128)), -1, 0)


def prep_shared(inp):
    f32 = np.float32
    mod_w = np.asarray(inp["mod_w"], f32)
    modw_t = np.ascontiguousarray(mod_w.reshape(2, KC, 128, 144, 128).transpose(0, 3, 2, 1, 4)).reshape(2, 144, 128, 16 * 128)
    w_in = np.asarray(inp["ffn_w_in"], f32)
    win_t = np.ascontiguousarray(w_in.reshape(2, 2, KC, 128, 2, JC, 128).transpose(0, 1, 5, 3, 2, 4, 6)).reshape(2, 2, JC, 128, 16 * 256)
    w_out = np.asarray(inp["ffn_w_out"], f32)
    wout_t = np.ascontiguousarray(w_out.reshape(2, 2, JC, 128, KC, 128).transpose(0, 1, 4, 3, 2, 5)).reshape(2, 2, KC, 128, JC * 128)
    r_in = np.asarray(inp["rec_w_in"], f32)[0]
    recin_t = np.ascontiguousarray(r_in.reshape(KC, 128, 2, KC, 128).transpose(3, 1, 0, 2, 4)).reshape(KC, 128, 16 * 256)
    r_out = np.asarray(inp["rec_w_out"], f32)[0]
    recout_t = np.ascontiguousarray(r_out.reshape(KC, 128, KC, 128).transpose(2, 1, 0, 3)).reshape(KC, 128, 16 * 128)
    f_out = np.asarray(inp["fou_w_out"], f32)[0]
    fouout_t = np.ascontiguousarray(f_out.reshape(KC, 128, KC, 128).transpose(2, 1, 0, 3)).reshape(KC, 128, 16 * 128)
    g_w = np.asarray(inp["rec_gate_w"], f32)[0]
    gw_t = np.ascontiguousarray(g_w.transpose(2, 3, 0, 1, 4)).reshape(KC, 128, 4 * 128)
    j = np.arange(256)
    ang = 2 * np.pi * ((j[:, None] * j[None, :]) % 256) / 256.0
    cc, sc = np.cos(ang) / 16.0, np.sin(ang) / 16.0
    cs = np.concatenate([cc, sc], axis=1).reshape(2, 128, 512).transpose(1, 0, 2).reshape(128, 1024)
    n = np.arange(SEQ)
    angn = 2 * np.pi * ((n[:, None] * n[None, :]) % SEQ) / float(SEQ)
    sq = 1.0 / math.sqrt(SEQ)
    cn, sn = np.cos(angn) * sq, -np.sin(angn) * sq
    dft = np.stack([cn, sn], 0).reshape(2, KC, 128, NT, 512).transpose(3, 0, 2, 1, 4).reshape(NT, 2, 128, 16 * 512)
    bf = ml_dtypes.bfloat16
    return dict(modw=modw_t, win=win_t, wout=wout_t, recin=recin_t, recout=recout_t, fouout=fouout_t, gw=gw_t,
                cs=np.ascontiguousarray(cs).astype(bf), dftn=np.ascontiguousarray(dft).astype(bf))


def prep_core(inp, b):
    f32 = np.float32
    x = np.asarray(inp["x"], f32)[b]
    ctx = np.asarray(inp["ctx"], f32)[b]
    vecs = np.zeros((128, NV), f32)
    cc = np.stack([np.asarray(inp["c"], f32)[b], np.asarray(inp["c_ctx"], f32)], 0)
    vecs[:, V_C:V_C + 32] = _fm(cc).transpose(0, 2, 1).reshape(128, 32)
    vecs[:, V_MODB:V_MODB + 288] = _fm(np.asarray(inp["mod_b"], f32).reshape(2, 9, D)).reshape(128, 288)
    vecs[:, V_NG:V_NG + 192] = _fm(np.asarray(inp["norm_g"], f32)).reshape(128, 192)
    vecs[:, V_GB:V_GB + 64] = _fm(np.asarray(inp["rec_gate_b"], f32)[0]).reshape(128, 64)
    vecs[:, V_LAM:V_LAM + 32] = _fm(np.asarray(inp["rec_lam"], f32)[0]).reshape(128, 32)
    vecs[:, V_CW:V_CW + 64] = _fm(np.asarray(inp["rec_conv_w"], f32)[0]).reshape(128, 64)
    vecs[:, V_CB:V_CB + 16] = _fm(np.asarray(inp["rec_conv_b"], f32)[0]).reshape(128, 16)
    return dict(xT=np.ascontiguousarray(x.T).reshape(KC, 128, SEQ),
                ctxT=np.ascontiguousarray(ctx.T).reshape(KC, 128, CTX), vecs=vecs)


_NC_CACHE = {}


def run(inputs, nsub=6, cores=8):
    if nsub not in _NC_CACHE:
        _NC_CACHE[nsub] = build(nsub)
    nc = _NC_CACHE[nsub]
    import time as _t
    _t0 = _t.time()
    shared = prep_shared(inputs)
    in_maps = []
    for b in range(cores):
        m = dict(shared)
        m.update(prep_core(inputs, b))
        in_maps.append(m)
    _t1 = _t.time()
    res = run_bass_kernel_spmd(nc, in_maps, core_ids=list(range(cores)))
    print(f"[kernel] prep {_t1 - _t0:.1f}s  spmd {_t.time() - _t1:.1f}s", flush=True)
    out = np.stack([np.ascontiguousarray(r["outT"].reshape(D, SEQ).T) for r in res.results], 0)
    return out.astype(np.float32)


def kernel(**inputs):
    return run(inputs, 6, 8)
```

```python
import math
from contextlib import ExitStack
import numpy as np
import ml_dtypes
import concourse.bass as bass
import concourse.mybir as mybir
from concourse.bass_utils import run_bass_kernel_spmd

F32 = mybir.dt.float32
BF16 = mybir.dt.bfloat16
AF = mybir.ActivationFunctionType
ALU = mybir.AluOpType

D = 2048
KC = 16
SEQ = 2048
TT = 512
NT = 4
CTX = 256
DFF = 5632
JC = 44
EPS = 1e-6
NV = 688
V_C, V_MODB, V_NG, V_GB, V_LAM, V_CW, V_CB = 0, 32, 320, 512, 576, 608, 672
SB_BASE = 16384 + 128
SB_LIMIT = 228992


class Res:
    __slots__ = ("name", "w", "r")

    def __init__(self, name=""):
        self.name = name
        self.w = None
        self.r = []


class Sched:
    def __init__(self, nc):
        self.nc = nc
        self.es = ExitStack()
        self.eng = {"pe": nc.tensor, "act": nc.scalar, "dve": nc.vector,
                    "pool": nc.gpsimd, "sp": nc.sync}
        self.sems = {}
        self.cnt = {}
        self.seen = {k: {} for k in self.eng}
        for k in ("pe", "act", "dve", "pool"):
            self._sem("E_" + k)
        self.sb_off = SB_BASE
        self.sb_top = SB_LIMIT
        self.uid = 0

    def _sem(self, key):
        if key not in self.sems:
            self.sems[key] = self.es.enter_context(self.nc.semaphore(key))
            self.cnt[key] = 0
        return self.sems[key]

    def sbuf(self, name, shape, dtype):
        esz = 2 if dtype == BF16 else 4
        nbytes = int(np.prod(shape[1:])) * esz
        nbytes = (nbytes + 63) // 64 * 64
        self.uid += 1
        t = self.nc.alloc_sbuf_tensor_at(f"{name}_{self.uid}", list(shape), dtype, offset=self.sb_off)
        self.sb_off += nbytes
        assert self.sb_off <= self.sb_top, (name, self.sb_off, self.sb_top)
        return t

    def sbuf_top(self, name, shape, dtype):
        esz = 2 if dtype == BF16 else 4
        nbytes = int(np.prod(shape[1:])) * esz
        nbytes = (nbytes + 63) // 64 * 64
        self.uid += 1
        self.sb_top -= nbytes
        assert self.sb_off <= self.sb_top, (name, self.sb_off, self.sb_top)
        return self.nc.alloc_sbuf_tensor_at(f"{name}_{self.uid}", list(shape), dtype, offset=self.sb_top)

    def soft_switch(self):
        for s in ("sp", "pool"):
            for k in ("pe", "act", "dve", "pool"):
                key = "E_" + k
                if self.cnt[key] > 0:
                    self._wait(s, (key, self.cnt[key]))

    def mark(self):
        return self.sb_off

    def reset(self, m):
        self.sb_off = m

    def _wait(self, stream, ev):
        if ev is None:
            return
        key, val = ev
        if self.seen[stream].get(key, 0) >= val:
            return
        self.seen[stream][key] = val
        self.eng[stream].wait_ge(self.sems[key], val)

    def _deps(self, stream, reads, writes):
        for r in reads:
            self._wait(stream, r.w)
        for w in writes:
            self._wait(stream, w.w)
            for ev in w.r:
                self._wait(stream, ev)

    def _commit(self, ev, reads, writes):
        for r in reads:
            r.r.append(ev)
        for w in writes:
            w.w = ev
            w.r = []

    def op(self, stream, fn, reads=(), writes=()):
        self._deps(stream, reads, writes)
        ins = fn(self.eng[stream])
        key = "E_" + stream
        self.cnt[key] += 1
        ins.then_inc(self.sems[key], 1)
        ev = (key, self.cnt[key])
        self._commit(ev, reads, writes)
        return ev

    def dma(self, stream, dsem, fn, reads=(), writes=()):
        self._deps(stream, reads, writes)
        key = "D_" + dsem
        sem = self._sem(key)
        inss = fn(self.eng[stream])
        if not isinstance(inss, (list, tuple)):
            inss = [inss]
        for ins in inss:
            ins.then_inc(sem, 16)
            self.cnt[key] += 16
        ev = (key, self.cnt[key])
        self._commit(ev, reads, writes)
        return ev

    def barrier(self):
        for s in self.eng:
            for key, c in self.cnt.items():
                if c > 0:
                    self._wait(s, (key, c))

    def close(self):
        self.es.close()


def rev_ap(ap):
    pat = [list(p) for p in ap.ap]
    assert len(pat) == 2, pat
    step, n = pat[1]
    return bass.AP(ap.tensor, ap.offset + step * (n - 1), [pat[0], [-step, n]])


class Ring:
    def __init__(self, S, name, n, shape, dtype):
        self.name = name
        self.n = n
        self.bufs = [S.sbuf(f"{name}{i}", shape, dtype) for i in range(n)]
        self.res = [Res(f"{name}{i}") for i in range(n)]
        self.i = 0

    def next(self):
        k = self.i % self.n
        self.i += 1
        return self.bufs[k], self.res[k], f"{self.name}{k}"


def mm_group(e, out, pairs):
    n = len(pairs)
    ins = None
    for i, (l, r) in enumerate(pairs):
        ins = e.matmul(out, l, r, start=(i == 0), stop=(i == n - 1))
    return ins


def build(nsub=6):
    nc = bass.Bass("TRN2", target_bir_lowering=False)
    dt_in = lambda name, shape, dt=F32: nc.dram_tensor(name, list(shape), dt, kind="ExternalInput").ap()
    xT = dt_in("xT", [KC, 128, SEQ])
    ctxT = dt_in("ctxT", [KC, 128, CTX])
    vecs_d = dt_in("vecs", [128, NV])
    modw = dt_in("modw", [2, 144, 128, 16 * 128])
    win = dt_in("win", [2, 2, JC, 128, 16 * 256])
    wout = dt_in("wout", [2, 2, KC, 128, JC * 128])
    recin = dt_in("recin", [KC, 128, 16 * 256])
    recout = dt_in("recout", [KC, 128, 16 * 128])
    fouout = dt_in("fouout", [KC, 128, 16 * 128])
    gw = dt_in("gw", [KC, 128, 4 * 128])
    cs_d = dt_in("cs", [128, 2 * 512], BF16)
    dftn = dt_in("dftn", [NT, 2, 128, 16 * 512], BF16)
    outT = nc.dram_tensor("outT", [KC, 128, SEQ], F32, kind="ExternalOutput").ap()
    xs = nc.dram_tensor("xs", [KC, 128, SEQ], F32, kind="Internal").ap()
    ctxs = nc.dram_tensor("ctxs", [KC, 128, CTX], F32, kind="Internal").ap()
    zs = nc.dram_tensor("zs", [KC, 128, SEQ], BF16, kind="Internal").ap()
    winc = nc.dram_tensor("winc", [2, 2, JC, 128, 16 * 256], BF16, kind="Internal").ap()
    woutc = nc.dram_tensor("woutc", [2, 2, KC, 128, JC * 128], BF16, kind="Internal").ap()
    mixc = nc.dram_tensor("mixc", [2, KC, 128, 16 * 128], BF16, kind="Internal").ap()

    S = Sched(nc)
    PA = S.es.enter_context(nc.psum_tensor("PA", [128, 2048], F32))
    PB = S.es.enter_context(nc.psum_tensor("PB", [128, 2048], F32))
    rPA = [Res(f"pa{i}") for i in range(4)]
    rPB = [Res(f"pb{i}") for i in range(4)]
    bank = lambda P, b, n=512: P[:, b * 512:b * 512 + n]

    def dres(name, ntile):
        return [[Res(f"{name}{t}_{m}") for m in range(KC)] for t in range(ntile)]
    r_xT, r_xs, r_out = dres("xT", NT), dres("xs", NT), dres("out", NT)
    r_ctxT, r_ctxs = dres("ctxT", 1), dres("ctxs", 1)
    r_zs = [Res(f"zs{c}") for c in range(KC)]

    vecs = S.sbuf("vecs", [128, NV], F32)
    mods = [S.sbuf(f"mods{l}", [128, 288], F32) for l in range(2)]
    der = S.sbuf("der", [128, 384], F32)
    ones = S.sbuf("ones", [128, 128], BF16)
    epsb = S.sbuf("epsb", [128, 1], F32)
    oneb = S.sbuf("oneb", [128, 1], F32)
    scb = S.sbuf("scb", [128, 32], BF16)
    coef = S.sbuf("coef", [128, 64], F32)
    r_vecs, r_mods, r_der, r_const, r_scb, r_coef = Res(), [Res(), Res()], Res(), Res(), Res(), Res()
    r_modps = [Res(f"modps{i}") for i in range(144)]

    S.dma("sp", "vecs", lambda e: e.dma_start(out=vecs[:], in_=vecs_d), writes=[r_vecs])
    S.op("dve", lambda e: e.memset(ones[:], 1.0), writes=[r_const])
    S.op("dve", lambda e: e.memset(epsb[:], EPS), writes=[r_const])
    S.op("dve", lambda e: e.memset(oneb[:], 1.0), writes=[r_const])

    def vcol(base, idx):
        return vecs[:, base + idx:base + idx + 1]

    def modcol(l, q, kc, which):
        i = (q * 16 + kc) * 2 + which
        return mods[l][:, i:i + 1]

    def modvec(l, q, which):
        t = mods[l]
        a = t[:, 0:1]
        return bass.AP(a.tensor, a.offset + q * 32 + which, [list(a.ap[0]), [2, 16]])

    def dercol(l, s, which, kind, kc):
        i = ((l * 3 + s) * 2 + which) * 32 + kind * 16 + kc
        return der[:, i:i + 1]

    def adaln_load(l, idx, wm):
        buf, rb, dn = wm.next()
        S.dma("pool", dn, lambda e: e.dma_start(out=buf[:], in_=modw[l, idx]), writes=[rb])
        return buf, rb

    def adaln_mm(idx, buf, rb):
        S.op("pe", lambda e: mm_group(e, PB[:, 1536 + idx * 2:1536 + idx * 2 + 2],
                                      [(buf[:, k * 128:(k + 1) * 128], scb[:, 2 * k:2 * k + 2]) for k in range(KC)]),
             reads=[rb, r_scb], writes=[r_modps[idx]])

    def adaln_tasks(l, idxs, wm, lag=2):
        pend = []
        tasks = []

        def mk_load(idx):
            def f():
                pend.append((idx,) + adaln_load(l, idx, wm))
            return f

        def mk_mm():
            def f():
                idx, buf, rb = pend.pop(0)
                adaln_mm(idx, buf, rb)
            return f
        seq = list(idxs)
        for i, idx in enumerate(seq):
            def both(idx=idx, i=i):
                mk_load(idx)()
                if i >= lag:
                    mk_mm()()
            tasks.append(both)
        for _ in range(min(lag, len(seq))):
            tasks.append(mk_mm())
        return tasks

    def adaln_evac(l, a, b):
        n = b - a
        for which in range(2):
            def ev(e, which=which):
                t = mods[l][:, 0:1]
                o = bass.AP(t.tensor, t.offset + 2 * a + which, [list(t.ap[0]), [2, n]])
                p = PB[:, 1536:1537]
                pi = bass.AP(p.tensor, p.offset + 2 * a + which, [list(p.ap[0]), [2, n]])
                return e.tensor_tensor(out=o, in0=pi, in1=vecs[:, V_MODB + l * 144 + a:V_MODB + l * 144 + b],
                                       op=ALU.add)
            S.op("dve", ev, reads=r_modps[a:b] + [r_vecs], writes=[r_mods[l]])

    def der_compute(l, s):
        q0 = 3 * s
        gc = 0.5 if s != 1 else 1.0
        for which in range(2):
            def da(e, which=which):
                i0 = ((l * 3 + s) * 2 + which) * 32
                return e.scalar_tensor_tensor(
                    out=der[:, i0:i0 + 16], in0=modvec(l, q0 + 1, which), scalar=1.0,
                    in1=vecs[:, V_NG + (l * 6 + 2 * s) * 16:V_NG + (l * 6 + 2 * s) * 16 + 16],
                    op0=ALU.add, op1=ALU.mult)
            S.op("dve", da, reads=[r_mods[l], r_vecs], writes=[r_der])

            def dg(e, which=which):
                i0 = ((l * 3 + s) * 2 + which) * 32 + 16
                return e.scalar_tensor_tensor(
                    out=der[:, i0:i0 + 16], in0=modvec(l, q0 + 2, which), scalar=gc,
                    in1=vecs[:, V_NG + (l * 6 + 2 * s + 1) * 16:V_NG + (l * 6 + 2 * s + 1) * 16 + 16],
                    op0=ALU.mult, op1=ALU.mult)
            S.op("dve", dg, reads=[r_mods[l], r_vecs], writes=[r_der])

    def adaln_start():
        S.op("act", lambda e: e.activation(out=scb[:], in_=vecs[:, V_C:V_C + 32], func=AF.Silu),
             reads=[r_vecs], writes=[r_scb])
        m0 = S.mark()
        wm = Ring(S, "wm", 6, [128, 16 * 128], BF16)
        for t in adaln_tasks(0, range(0, 48), wm, lag=4):
            t()
        adaln_evac(0, 0, 48)
        der_compute(0, 0)
        S.barrier()
        S.reset(m0)

    class Work:
        pass

    def alloc_common(W, n_xinA=3):
        W.xy = S.sbuf("xy", [128, KC, TT], F32)
        W.r_xy = [Res(f"xy{m}") for m in range(KC)]
        W.rs = S.sbuf("rs", [128, TT], F32)
        W.r_rs = Res("rs")
        W.rsq = S.sbuf("rsq", [128, TT], F32)
        W.r_rsq = Res("rsq")
        W.tmp = Ring(S, "tmp", 2, [128, TT], F32)
        W.sq2 = Ring(S, "sq2", 2, [128, TT], BF16)
        W.xin = Ring(S, "xin", 3, [128, TT], F32)
        W.xinA = Ring(S, "xinA", n_xinA, [128, TT], F32)
        W.hb = S.sbuf("hb", [128, KC, TT], BF16)
        W.r_hb = [Res(f"hb{k}") for k in range(KC)]
        W.pre_alt = 0
        if n_xinA > 3:
            W.rs2 = S.sbuf("rs2", [128, TT], F32)
            W.r_rs2 = Res("rs2")

    M_BASE = S.mark()
    WC = Work()
    alloc_common(WC)
    P_MARK = S.mark()

    def rstd_from_stat(W, T):
        S.op("act", lambda e: e.activation(out=W.rs[:, :T], in_=bank(PB, 2, T), func=AF.Sqrt,
                                           bias=epsb[:], scale=1.0 / D),
             reads=[rPB[2], r_const], writes=[W.r_rs])
        S.op("dve", lambda e: e.reciprocal(out=W.rs[:, :T], in_=W.rs[:, :T]), reads=[W.r_rs], writes=[W.r_rs])

    def pre_phase(W, src, r_src, t0, T, l, s, which, dst, r_dst):
        S.dma("sp", "xy", lambda e: e.dma_start(out=W.xy[:, :, :T],
                                                in_=src[:, :, t0:t0 + T].rearrange("c p t -> p c t")),
              reads=r_src, writes=W.r_xy)
        for kc in range(KC):
            S.op("act", lambda e, kc=kc: e.activation(out=dst(kc), in_=W.xy[:, kc, :T], func=AF.Square),
                 reads=[W.r_xy[kc]], writes=[r_dst(kc)])
        S.op("pe", lambda e: mm_group(e, bank(PB, 2, T), [(ones[:], dst(kc)) for kc in range(KC)]),
             reads=[r_const] + [r_dst(kc) for kc in range(KC)], writes=[rPB[2]])
        rstd_from_stat(W, T)
        for kc in range(KC):
            tb, rt, _ = W.tmp.next()
            S.op("dve", lambda e, kc=kc, tb=tb: e.tensor_tensor(out=tb[:, :T], in0=W.xy[:, kc, :T],
                                                                 in1=W.rs[:, :T], op=ALU.mult),
                 reads=[W.r_xy[kc], W.r_rs], writes=[rt])
            S.op("act", lambda e, kc=kc, tb=tb: e.activation(
                out=dst(kc), in_=tb[:, :T], func=AF.Identity,
                bias=modcol(l, 3 * s, kc, which), scale=dercol(l, s, which, 0, kc)),
                reads=[rt, r_der, r_mods[l]], writes=[r_dst(kc)])

    def y_phase(W, wtile, nk, rhs, r_rhs, T, wo, cache=None, inter=None):
        pend = None
        inter = list(inter) if inter else []
        n_inter = len(inter)
        for m in range(KC):
            want_left = n_inter - (n_inter * (m + 1)) // KC
            while len(inter) > want_left:
                inter.pop(0)()
            buf, rb, dn = wo.next()
            if cache is not None and not cache[2]:
                S.dma("pool", dn, lambda e, buf=buf, m=m: e.dma_start(out=buf[:, :nk * 128], in_=wtile(m)),
                      reads=[cache[1][m]], writes=[rb])
            else:
                S.dma("pool", dn, lambda e, buf=buf, m=m: e.dma_start(
                    out=buf[:, :nk * 128], in_=wtile(m), max_dma_last_dim=8192), writes=[rb])
                if cache is not None:
                    S.dma("sp", dn + "s", lambda e, buf=buf, m=m: e.dma_start(out=cache[0](m), in_=buf[:, :nk * 128]),
                          reads=[rb], writes=[cache[1][m]])
            pb = m % 2
            S.op("pe", lambda e, buf=buf, pb=pb: mm_group(
                e, bank(PB, pb, T), [(buf[:, k * 128:(k + 1) * 128], rhs(k)) for k in range(nk)]),
                reads=[rb] + r_rhs, writes=[rPB[pb]])
            if pend is not None:
                pend()
            S.op("act", lambda e, m=m, pb=pb: e.activation(out=W.xy[:, m, :T], in_=bank(PB, pb, T), func=AF.Copy),
                 reads=[rPB[pb]], writes=[W.r_xy[m]])
            sb, rsq, _ = W.sq2.next()
            S.op("act", lambda e, sb=sb, pb=pb: e.activation(out=sb[:, :T], in_=bank(PB, pb, T), func=AF.Square),
                 reads=[rPB[pb]], writes=[rsq])

            def stat(m=m, sb=sb, rsq=rsq):
                S.op("pe", lambda e: e.matmul(bank(PB, 2, T), ones[:], sb[:, :T], start=(m == 0), stop=(m == KC - 1)),
                     reads=[rsq, r_const], writes=[rPB[2]])
            pend = stat
        pend()

    def post_steps(W, src, r_src, dst, r_dstd, t0, T, l, s, which):
        loads = {}
        dpp = W.xin.n - 3 if W.xin.n >= 6 else W.xin.n - 1

        def load(m):
            xb, rx, dn = W.xin.next()
            S.dma("sp", dn, lambda e: e.dma_start(out=xb[:, :T], in_=src[m, :, t0:t0 + T]),
                  reads=[r_src[m]], writes=[rx])
            loads[m] = (xb, rx, dn)

        def step_r():
            S.op("act", lambda e: e.activation(out=W.rsq[:, :T], in_=bank(PB, 2, T), func=AF.Sqrt,
                                               bias=epsb[:], scale=1.0 / D),
                 reads=[rPB[2], r_const], writes=[W.r_rsq])
            S.op("dve", lambda e: e.reciprocal(out=W.rsq[:, :T], in_=W.rsq[:, :T]), reads=[W.r_rsq],
                 writes=[W.r_rsq])
            for i in range(dpp):
                load(i)

        def mk(m):
            def f():
                if m + dpp < KC:
                    load(m + dpp)
                xb, rx, dn = loads.pop(m)
                S.op("dve", lambda e: e.tensor_tensor(out=W.xy[:, m, :T], in0=W.xy[:, m, :T], in1=W.rsq[:, :T],
                                                      op=ALU.mult),
                     reads=[W.r_rsq], writes=[W.r_xy[m]])
                S.op("dve", lambda e: e.scalar_tensor_tensor(
                    out=xb[:, :T], in0=W.xy[:, m, :T], scalar=dercol(l, s, which, 1, m), in1=xb[:, :T],
                    op0=ALU.mult, op1=ALU.add), reads=[W.r_xy[m], r_der], writes=[rx])
                S.dma("sp", dn + "s", lambda e: e.dma_start(out=dst[m, :, t0:t0 + T], in_=xb[:, :T]),
                      reads=[rx], writes=[r_dstd[m]])
            return f
        return [step_r] + [mk(m) for m in range(KC)]

    def post_phase(W, src, r_src, dst, r_dstd, t0, T, l, s, which):
        for st in post_steps(W, src, r_src, dst, r_dstd, t0, T, l, s, which):
            st()

    def pre_steps(W, src, r_src, t0, T, l, s, which, dst, r_dst, alt=0):
        loads = {}
        dp = 3 if W.xinA.n >= 10 else 2
        sb_ = alt
        rs_, r_rs_ = (W.rs, W.r_rs) if alt == 0 else (W.rs2, W.r_rs2)

        def load(tag, kc):
            xb, rx, dn = W.xinA.next()
            S.dma("sp", dn, lambda e: e.dma_start(out=xb[:, :T], in_=src[kc, :, t0:t0 + T]),
                  reads=[r_src[kc]], writes=[rx])
            loads[(tag, kc)] = (xb, rx)

        def mkA(kc):
            def f():
                if kc == 0:
                    for i in range(dp):
                        load("A", i)
                if kc + dp < KC:
                    load("A", kc + dp)
                xb, rx = loads.pop(("A", kc))
                S.op("act", lambda e: e.activation(out=dst(kc), in_=xb[:, :T], func=AF.Square),
                     reads=[rx], writes=[r_dst(kc)])
                S.op("pe", lambda e: e.matmul(bank(PA, sb_, T), ones[:], dst(kc), start=(kc == 0), stop=(kc == KC - 1)),
                     reads=[r_const, r_dst(kc)], writes=[rPA[sb_]])
            return f

        def step_r():
            for i in range(dp):
                load("B", i)
            S.op("act", lambda e: e.activation(out=rs_[:, :T], in_=bank(PA, sb_, T), func=AF.Sqrt,
                                               bias=epsb[:], scale=1.0 / D),
                 reads=[rPA[sb_], r_const], writes=[r_rs_])
            S.op("dve", lambda e: e.reciprocal(out=rs_[:, :T], in_=rs_[:, :T]), reads=[r_rs_], writes=[r_rs_])

        def mkB(kc):
            def f():
                if kc + dp < KC:
                    load("B", kc + dp)
                xb, rx = loads.pop(("B", kc))
                S.op("dve", lambda e: e.tensor_tensor(out=xb[:, :T], in0=xb[:, :T], in1=rs_[:, :T], op=ALU.mult),
                     reads=[r_rs_], writes=[rx])
                S.op("act", lambda e: e.activation(
                    out=dst(kc), in_=xb[:, :T], func=AF.Identity,
                    bias=modcol(l, 3 * s, kc, which), scale=dercol(l, s, which, 0, kc)),
                    reads=[rx, r_der, r_mods[l]], writes=[r_dst(kc)])
            return f
        return [mkA(kc) for kc in range(KC)] + [step_r] + [mkB(kc) for kc in range(KC)]

    def pre_pipeline(W, jobs_args, carry=None):
        lists = [pre_steps(W, *a, alt=i % 2) for i, a in enumerate(jobs_args)]
        carry = list(carry) if carry else []
        for st in lists[0][:KC]:
            st()
            if carry:
                carry.pop(0)()
        for i, L in enumerate(lists):
            L[KC]()
            nxt = lists[i + 1][:KC] if i + 1 < len(lists) else []
            for k in range(KC):
                L[KC + 1 + k]()
                if carry:
                    carry.pop(0)()
                if nxt:
                    nxt[k]()
        for st in carry:
            st()

    cache_res = {}

    def get_cache_res(l, f):
        if (l, f) not in cache_res:
            cache_res[(l, f)] = dict(win=[Res(f"winc{l}{f}_{j}") for j in range(JC)],
                                     wout=[Res(f"woutc{l}{f}_{m}") for m in range(KC)], pre=False)
        return cache_res[(l, f)]

    pc_slots = [Res(f"pcslot{i}") for i in range(8)]
    pc_i = [0]

    def preconv_tasks(l, f):
        cr = get_cache_res(l, f)
        cr["pre"] = True
        tasks = []

        def mk(src, dstc, r):
            def t():
                k = pc_i[0] % len(pc_slots)
                pc_i[0] += 1
                S.dma("pool", f"pc{k}", lambda e: e.dma_start(out=dstc, in_=src, max_dma_last_dim=8192),
                      writes=[r, pc_slots[k]])
            return t
        for j in range(JC):
            tasks.append(mk(win[l, f, j], winc[l, f, j], cr["win"][j]))
        for m in range(KC):
            tasks.append(mk(wout[l, f, m], woutc[l, f, m], cr["wout"][m]))
        return tasks

    def ffn_first_pre(l, f, job):
        (src, r_src, dst, r_dd, t0, T, ti, which) = job
        s_ = 0 if f == 0 else 2
        return pre_steps(WC, src, r_src[ti], t0, T, l, s_, which,
                         lambda kc, T=T: WC.hb[:, kc, :T], lambda kc: WC.r_hb[kc])

    def ffn(l, f, jobs, bg_l=None, bg_idxs=(), carry=None, defer=False, pre_done=False, before_last_y=None,
            next_pre=None):
        s = 0 if f == 0 else 2
        S.soft_switch()
        m0 = S.mark()
        W = WC
        wi = Ring(S, "wi", 5, [128, 16 * 256], BF16)
        wo = Ring(S, "wo", 3, [128, JC * 128], BF16)
        wm = Ring(S, "wm", 3, [128, 16 * 128], BF16)
        bg = adaln_tasks(bg_l, bg_idxs, wm, lag=2) if bg_l is not None else []
        n_bg_iters = max(1, (len(jobs) - 1) * JC)
        bg_done = 0
        bg_iter = 0
        cr = get_cache_res(l, f)
        r_winc, r_woutc, pre_conv = cr["win"], cr["wout"], cr["pre"]
        hb, r_hb = W.hb, W.r_hb
        hid = S.sbuf("hid", [128, JC, TT], BF16)
        r_hid = [Res(f"hid{j}") for j in range(JC)]
        def mk_pre(job):
            (src, r_src, dst, r_dd, t0, T, ti, which) = job
            return pre_steps(W, src, r_src[ti], t0, T, l, s, which,
                             lambda kc, T=T: hb[:, kc, :T], lambda kc: r_hb[kc])

        def mk_post(job):
            (src, r_src, dst, r_dd, t0, T, ti, which) = job
            return post_steps(W, src, r_src[ti], dst, r_dd[ti], t0, T, l, s, which)
        if not pre_done:
            for st in mk_pre(jobs[0]):
                st()
        posts = list(carry) if carry else []
        for ji, (src, r_src, dst, r_dd, t0, T, ti, which) in enumerate(jobs):
            n_post = len(posts)
            for j in range(JC):
                want_left = n_post - (n_post * (j + 1)) // JC
                while len(posts) > want_left:
                    posts.pop(0)()
                buf, rb, dn = wi.next()
                if ji == 0 and not pre_conv:
                    S.dma("pool", dn, lambda e, buf=buf, j=j: e.dma_start(
                        out=buf[:], in_=win[l, f, j], max_dma_last_dim=8192), writes=[rb])
                    S.dma("sp", dn + "s", lambda e, buf=buf, j=j: e.dma_start(out=winc[l, f, j], in_=buf[:]),
                          reads=[rb], writes=[r_winc[j]])
                else:
                    S.dma("pool", dn, lambda e, buf=buf, j=j: e.dma_start(out=buf[:], in_=winc[l, f, j]),
                          reads=[r_winc[j]], writes=[rb])
                    if ji > 0:
                        bg_iter += 1
                        want = (len(bg) + bg_done) * bg_iter // n_bg_iters
                        while bg and bg_done < want:
                            bg.pop(0)()
                            bg_done += 1
                pb = j % 2

                def pe(e, buf=buf, pb=pb):
                    mm_group(e, bank(PA, pb, T), [(buf[:, k * 256:k * 256 + 128], hb[:, k, :T]) for k in range(KC)])
                    return mm_group(e, bank(PA, 2 + pb, T),
                                    [(buf[:, k * 256 + 128:k * 256 + 256], hb[:, k, :T]) for k in range(KC)])
                S.op("pe", pe, reads=[rb] + r_hb, writes=[rPA[pb], rPA[2 + pb]])
                tb, rt, _ = W.tmp.next()
                S.op("act", lambda e, tb=tb, pb=pb: e.activation(out=tb[:, :T], in_=bank(PA, pb, T), func=AF.Silu),
                     reads=[rPA[pb]], writes=[rt])
                S.op("dve", lambda e, tb=tb, pb=pb, j=j: e.tensor_tensor(
                    out=hid[:, j, :T], in0=tb[:, :T], in1=bank(PA, 2 + pb, T), op=ALU.mult),
                    reads=[rt, rPA[2 + pb]], writes=[r_hid[j]])
            if ji + 1 < len(jobs):
                inter = mk_pre(jobs[ji + 1])
            else:
                while bg:
                    bg.pop(0)()
                if before_last_y is not None:
                    before_last_y()
                inter = next_pre() if next_pre is not None else None
            if ji == 0 and not pre_conv:
                y_phase(W, lambda m: wout[l, f, m], JC, lambda k: hid[:, k, :T], r_hid, T, wo,
                        cache=(lambda m: woutc[l, f, m], r_woutc, True), inter=inter)
            else:
                y_phase(W, lambda m: woutc[l, f, m], JC, lambda k: hid[:, k, :T], r_hid, T, wo,
                        cache=(None, r_woutc, False), inter=inter)
            posts = mk_post(jobs[ji])
        while bg:
            bg.pop(0)()
        S.reset(m0)
        if defer:
            return posts
        for st in posts:
            st()
        S.barrier()
        return None

    def mixer_out(l, wsrc, from_zs, hx=None, r_hx=None, dst=None, r_dst=None, next_pre=None):
        W = WC
        W2 = Work()
        W2.__dict__.update(W.__dict__)
        W2.xy = S.sbuf("xyb", [128, KC, TT], F32)
        W2.r_xy = [Res(f"xyb{m}") for m in range(KC)]
        W2.rsq = S.sbuf("rsqb", [128, TT], F32)
        W2.r_rsq = Res("rsqb")
        W2.xin = Ring(S, "xinb", 8, [128, TT], F32)
        W3 = Work()
        W3.__dict__.update(W.__dict__)
        W3.xin = W2.xin
        Ws = [W2, W3, W2, W]
        wo = Ring(S, "wo", 3, [128, KC * 128], BF16)
        r_mixc = [Res(f"mixc{m}") for m in range(KC)]
        if from_zs:
            hbr = Ring(S, "hbz", 2, [128, KC, TT], BF16)
        posts = None
        hbq = {}

        def load_hbz(t):
            if from_zs and t < NT:
                hb, r_hb1, dn = hbr.next()
                S.dma("sp", dn, lambda e: e.dma_start(
                    out=hb[:], in_=zs[:, :, t * TT:(t + 1) * TT].rearrange("c p t -> p c t")),
                    reads=r_zs, writes=[r_hb1])
                hbq[t] = (hb, r_hb1)
        load_hbz(0)
        for t in range(NT):
            t0 = t * TT
            Wt = Ws[t]
            load_hbz(t + 1)
            if from_zs:
                hb, r_hb1 = hbq.pop(t)
                rhs = lambda k, hb=hb: hb[:, k, :]
                rr = [r_hb1]
            else:
                rhs = lambda k, t0=t0: hx[:, k, t0:t0 + TT]
                rr = r_hx
            if t == NT - 1 and next_pre is not None:
                posts = list(posts) + next_pre()
            if t == 0:
                y_phase(Wt, lambda m: wsrc[m], KC, rhs, rr, TT, wo, inter=posts,
                        cache=(lambda m: mixc[l, m], r_mixc, True))
            else:
                y_phase(Wt, lambda m: mixc[l, m], KC, rhs, rr, TT, wo, inter=posts, cache=(None, r_mixc, False))
            posts = post_steps(Wt, xs, r_xs[t], dst, r_dst[t], t0, TT, l, 1, 0)
        return posts

    def mixer_pre_work():
        W = Work()
        W.__dict__.update(WC.__dict__)
        W.xinA = Ring(S, "xinB", 10, [128, TT], F32)
        W.rs2 = S.sbuf("rs2", [128, TT], F32)
        W.r_rs2 = Res("rs2")
        return W

    def rglru(carry=None, next_pre=None):
        l = 0
        S.soft_switch()
        m0 = S.mark()
        top0 = S.sb_top
        hx = S.sbuf_top("hx", [128, KC, SEQ], BF16)
        r_hx = [Res(f"hx{k}") for k in range(KC)]
        hcb = S.sbuf_top("hcb", [128, KC, CTX], BF16)
        r_hcb = [Res(f"hcb{k}") for k in range(KC)]
        W = mixer_pre_work()
        r_hxt = [[Res(f"hx{t}_{k}") for k in range(KC)] for t in range(NT)]
        pre_pipeline(W, [(xs, r_xs[t], t * TT, TT, l, 1, 0,
                          (lambda kc, t=t: hx[:, kc, t * TT:(t + 1) * TT]), (lambda kc, t=t: r_hxt[t][kc]))
                         for t in range(NT)] +
                     [(ctxs, r_ctxs[0], 0, CTX, l, 1, 1, (lambda kc: hcb[:, kc, :]), (lambda kc: r_hcb[kc]))],
                     carry=carry)
        S.barrier()
        S.reset(M_BASE)
        wi = Ring(S, "wi", 2, [128, 16 * 256], BF16)
        gwr = Ring(S, "gwr", 3, [128, 4 * 128], BF16)
        vp = S.sbuf("vp", [128, 32, 67], F32)
        vc = S.sbuf("vc", [128, SEQ], F32)
        vcb = S.sbuf("vcb", [128, SEQ], BF16)
        bA = S.sbuf("bA", [128, SEQ], F32)
        bB = [S.sbuf(f"bB{d}", [128, SEQ], F32) for d in range(2)]
        bC = [S.sbuf(f"bC{d}", [128, SEQ], F32) for d in range(2)]
        bH = [S.sbuf(f"bH{d}", [128, SEQ], F32) for d in range(2)]
        bG = [S.sbuf(f"bG{i}", [128, SEQ], F32) for i in range(2)]
        zb = Ring(S, "zb", 1, [128, SEQ], BF16)
        vpc = [S.sbuf(f"vpc{i}", [128, CTX + 3], F32) for i in range(2)]
        vcc = [S.sbuf(f"vcc{i}", [128, CTX], F32) for i in range(2)]
        vccb = [S.sbuf(f"vccb{i}", [128, CTX], BF16) for i in range(2)]
        cA = S.sbuf("cA", [128, CTX], F32)
        cB = [S.sbuf(f"cB{d}", [128, CTX], F32) for d in range(2)]
        cC = [S.sbuf(f"cC{d}", [128, CTX], F32) for d in range(2)]
        cH = S.sbuf("cH", [128, 2, CTX], F32)
        hgb = S.sbuf("hgb", [128, 64], F32)
        r_vp, r_vc, r_vcb = Res(), Res(), Res()
        r_G = [Res(), Res()]
        HALF = SEQ // 2
        r_A = [Res(), Res()]
        r_B = [[Res(), Res()] for d in range(2)]
        r_C = [Res(), Res()]
        r_H = [Res(), Res()]
        r_vpc, r_vcc, r_vccb = [Res(), Res()], [Res(), Res()], [Res(), Res()]
        r_cA, r_cB, r_cC, r_cH = Res(), [Res(), Res()], [Res(), Res()], [Res(), Res()]
        S.op("dve", lambda e: e.memset(vp[:], 0.0), writes=[r_vp])
        for i in range(2):
            S.op("dve", lambda e, i=i: e.memset(vpc[i][:], 0.0), writes=[r_vpc[i]])
        S.op("act", lambda e: e.activation(out=coef[:, 0:32], in_=vecs[:, V_LAM:V_LAM + 32], func=AF.Exp, scale=-1.0),
             reads=[r_vecs], writes=[r_coef])
        S.op("act", lambda e: e.activation(out=coef[:, 0:32], in_=coef[:, 0:32], func=AF.Ln, bias=oneb[:], scale=1.0),
             reads=[r_coef, r_const], writes=[r_coef])
        S.op("dve", lambda e: e.tensor_scalar(out=coef[:, 32:64], in0=coef[:, 0:32], scalar1=-4.0, scalar2=None,
                                              op0=ALU.mult), reads=[r_coef], writes=[r_coef])
        S.op("dve", lambda e: e.tensor_scalar(out=coef[:, 0:32], in0=coef[:, 0:32], scalar1=-8.0, scalar2=None,
                                              op0=ALU.mult), reads=[r_coef], writes=[r_coef])
        S.op("dve", lambda e: e.tensor_scalar(out=hgb[:], in0=vecs[:, V_GB:V_GB + 64], scalar1=0.5, scalar2=None,
                                              op0=ALU.mult), reads=[r_vecs], writes=[r_coef])
        pv3 = PA[:, :].rearrange("p (r w) -> p r w", w=64)
        vc3 = vc[:, :].rearrange("p (r w) -> p r w", w=64)

        def tanh_pair(d, c, ps_r, rps_r, ps_i, rps_i, A, rA, B, rB):
            S.op("act", lambda e: e.activation(out=A, in_=ps_r, func=AF.Tanh, scale=0.5,
                                               bias=hgb[:, (d * 2 + 0) * 16 + c:(d * 2 + 0) * 16 + c + 1]),
                 reads=rps_r + [r_coef], writes=rA)
            S.op("act", lambda e: e.activation(out=B, in_=ps_i, func=AF.Tanh, scale=0.5,
                                               bias=hgb[:, (d * 2 + 1) * 16 + c:(d * 2 + 1) * 16 + c + 1]),
                 reads=rps_i + [r_coef], writes=rB)

        wslots = {}

        def load_w(c):
            if c >= KC:
                return
            buf, rb, dn = wi.next()
            S.dma("pool", dn, lambda e: e.dma_start(out=buf[:], in_=recin[c], max_dma_last_dim=8192), writes=[rb])
            gb, rgb, gdn = gwr.next()
            S.dma("pool", gdn, lambda e: e.dma_start(out=gb[:], in_=gw[c]), writes=[rgb])
            wslots[c] = (buf, rb, gb, rgb)

        def emit_ctxv(c):
            if c >= KC:
                return
            buf, rb, gb, rgb = wslots[c]
            p = c % 2
            S.op("pe", lambda e: mm_group(e, bank(PA, 0, CTX), [(buf[:, k * 256 + 128:k * 256 + 256], hcb[:, k, :])
                                                                for k in range(KC)]),
                 reads=[rb] + r_hcb, writes=[rPA[0]])
            S.op("act", lambda e: e.activation(out=vpc[p][:, 2:2 + CTX], in_=bank(PA, 0, CTX), func=AF.Copy),
                 reads=[rPA[0]], writes=[r_vpc[p]])

        def emit_v(c):
            if c >= KC:
                return
            buf, rb, gb, rgb = wslots[c]
            for tt in range(NT):
                S.op("pe", lambda e, tt=tt: mm_group(e, bank(PA, tt), [
                    (buf[:, k * 256 + 128:k * 256 + 256], hx[:, k, tt * TT:(tt + 1) * TT]) for k in range(KC)]),
                    reads=[rb] + r_hx, writes=[rPA[tt]])

        def stage1(c):
            if c >= KC:
                return
            buf, rb, gb, rgb = wslots[c]
            p = c % 2
            S.op("act", lambda e: e.activation(out=vp[:, :, 2:66], in_=pv3, func=AF.Copy), reads=rPA, writes=[r_vp])
            for tt in range(NT):
                S.op("pe", lambda e, tt=tt: mm_group(e, bank(PA, tt), [
                    (buf[:, k * 256:k * 256 + 128], hx[:, k, tt * TT:(tt + 1) * TT]) for k in range(KC)]),
                    reads=[rb] + r_hx, writes=[rPA[tt]])
            load_w(c + 2)
            S.op("act", lambda e: e.activation(out=bG[p][:], in_=PA[:, :], func=AF.Gelu_apprx_tanh),
                 reads=rPA, writes=[r_G[p]])
            cw = lambda k: vcol(V_CW, k * 16 + c)
            S.op("dve", lambda e: e.tensor_scalar(out=vcc[p][:], in0=vpc[p][:, 2:2 + CTX], scalar1=cw(2),
                                                  scalar2=vcol(V_CB, c), op0=ALU.mult, op1=ALU.add),
                 reads=[r_vpc[p], r_vecs], writes=[r_vcc[p]])
            for k, o in ((0, 0), (1, 1), (3, 3)):
                S.op("dve", lambda e, k=k, o=o: e.scalar_tensor_tensor(
                    out=vcc[p][:], in0=vpc[p][:, o:o + CTX], scalar=cw(k), in1=vcc[p][:], op0=ALU.mult, op1=ALU.add),
                    reads=[r_vpc[p], r_vcc[p], r_vecs], writes=[r_vcc[p]])
            S.op("dve", lambda e: e.tensor_copy(out=vccb[p][:], in_=vcc[p][:]), reads=[r_vcc[p]], writes=[r_vccb[p]])
            S.op("dve", lambda e: e.tensor_scalar(out=vc3, in0=vp[:, :, 2:66], scalar1=cw(2),
                                                  scalar2=vcol(V_CB, c), op0=ALU.mult, op1=ALU.add),
                 reads=[r_vp, r_vecs], writes=[r_vc])
            for k, o in ((0, 0), (1, 1), (3, 3)):
                S.op("dve", lambda e, k=k, o=o: e.scalar_tensor_tensor(
                    out=vc3, in0=vp[:, :, o:o + 64], scalar=cw(k), in1=vc3, op0=ALU.mult, op1=ALU.add),
                    reads=[r_vp, r_vc, r_vecs], writes=[r_vc])
            S.op("dve", lambda e: e.tensor_copy(out=vcb[:], in_=vc[:]), reads=[r_vc], writes=[r_vcb])
            emit_ctxv(c + 1)

        load_w(0)
        load_w(1)
        emit_ctxv(0)
        emit_v(0)
        stage1(0)
        pct = preconv_tasks(0, 1) if nsub >= 3 else []
        n_pct = len(pct)
        for c in range(KC):
            buf, rb, gb, rgb = wslots[c]
            p = c % 2
            while len(pct) > n_pct - (n_pct * (c + 1)) // KC:
                pct.pop(0)()
            gcol = lambda d, g: hgb[:, (d * 2 + g) * 16 + c:(d * 2 + g) * 16 + c + 1]
            chc = lambda d: coef[:, 32 + d * 16 + c:32 + d * 16 + c + 1]
            def pe_c(e):
                ins = None
                for d in range(2):
                    e.matmul(PB[:, d * 512:d * 512 + CTX], gb[:, (d * 2) * 128:(d * 2 + 1) * 128], vccb[p][:],
                             start=True, stop=True)
                    ins = e.matmul(PB[:, d * 512 + CTX:d * 512 + 2 * CTX], gb[:, (d * 2 + 1) * 128:(d * 2 + 2) * 128],
                                   vccb[p][:], start=True, stop=True)
                return ins
            S.op("pe", pe_c, reads=[rgb, r_vccb[p]], writes=[rPB[0], rPB[1]])
            for d in range(2):
                S.op("act", lambda e, d=d: e.activation(out=cA[:], in_=PB[:, d * 512:d * 512 + CTX], func=AF.Tanh,
                                                        scale=0.5, bias=gcol(d, 0)),
                     reads=[rPB[d], r_coef], writes=[r_cA])
                S.op("act", lambda e, d=d: e.activation(out=cB[d][:], in_=PB[:, d * 512 + CTX:d * 512 + 2 * CTX],
                                                        func=AF.Tanh, scale=0.5, bias=gcol(d, 1)),
                     reads=[rPB[d], r_coef], writes=[r_cB[d]])
                S.op("act", lambda e, d=d: e.activation(out=cC[d][:], in_=cA[:], func=AF.Exp, scale=chc(d), bias=chc(d)),
                     reads=[r_cA, r_coef], writes=[r_cC[d]])
            for d in range(2):
                gwa = gb[:, (d * 2 + 0) * 128:(d * 2 + 1) * 128]
                gwx = gb[:, (d * 2 + 1) * 128:(d * 2 + 2) * 128]
                for hf in range(2):
                    lo, hi = hf * HALF, (hf + 1) * HALF

                    def pe_l(e, gwa=gwa, gwx=gwx, lo=lo):
                        for t2 in range(2):
                            e.matmul(bank(PB, t2), gwa, vcb[:, lo + t2 * TT:lo + (t2 + 1) * TT], start=True, stop=True)
                        ins = None
                        for t2 in range(2):
                            ins = e.matmul(bank(PB, 2 + t2), gwx, vcb[:, lo + t2 * TT:lo + (t2 + 1) * TT],
                                           start=True, stop=True)
                        return ins
                    S.op("pe", pe_l, reads=[rgb, r_vcb], writes=rPB)
                    S.op("act", lambda e, d=d, lo=lo, hi=hi: e.activation(
                        out=bA[:, lo:hi], in_=PB[:, 0:HALF], func=AF.Tanh, scale=0.5, bias=gcol(d, 0)),
                        reads=rPB[0:2] + [r_coef], writes=[r_A[hf]])
                    S.op("act", lambda e, d=d, lo=lo, hi=hi: e.activation(
                        out=bB[d][:, lo:hi], in_=PB[:, HALF:SEQ], func=AF.Tanh, scale=0.5, bias=gcol(d, 1)),
                        reads=rPB[2:4] + [r_coef], writes=[r_B[d][hf]])
                S.op("act", lambda e, d=d: e.activation(out=bC[d][:], in_=bA[:], func=AF.Exp, scale=chc(d), bias=chc(d)),
                     reads=r_A + [r_coef], writes=[r_C[d]])
            emit_v(c + 1)
            for d in range(2):
                S.op("dve", lambda e, d=d: e.tensor_tensor(out=cH[:, d, :], in0=cC[d][:], in1=cC[d][:], op=ALU.mult),
                     reads=[r_cC[d]], writes=[r_cH[d]])
                S.op("dve", lambda e, d=d: e.tensor_tensor(out=bH[d][:], in0=bC[d][:], in1=bC[d][:], op=ALU.mult),
                     reads=[r_C[d]], writes=[r_H[d]])
            for d in range(2):
                S.op("act", lambda e, d=d: e.activation(out=cH[:, d, :], in_=cH[:, d, :], func=AF.Sqrt, bias=oneb[:],
                                                        scale=-1.0), reads=[r_cH[d], r_const], writes=[r_cH[d]])
                S.op("act", lambda e, d=d: e.activation(out=bH[d][:], in_=bH[d][:], func=AF.Sqrt, bias=oneb[:],
                                                        scale=-1.0), reads=[r_H[d], r_const], writes=[r_H[d]])
            for d in range(2):
                S.op("dve", lambda e, d=d: e.scalar_tensor_tensor(out=cB[d][:], in0=cB[d][:], scalar=1.0, in1=cH[:, d, :],
                                                                  op0=ALU.add, op1=ALU.mult),
                     reads=[r_cH[d], r_cB[d]], writes=[r_cB[d]])
                S.op("dve", lambda e, d=d: e.scalar_tensor_tensor(out=cB[d][:], in0=cB[d][:], scalar=0.5, in1=vcc[p][:],
                                                                  op0=ALU.mult, op1=ALU.mult),
                     reads=[r_cB[d], r_vcc[p]], writes=[r_cB[d]])
                S.op("dve", lambda e, d=d: e.scalar_tensor_tensor(out=bB[d][:], in0=bB[d][:], scalar=1.0, in1=bH[d][:],
                                                                  op0=ALU.add, op1=ALU.mult),
                     reads=[r_H[d]] + r_B[d], writes=r_B[d])
                S.op("dve", lambda e, d=d: e.scalar_tensor_tensor(out=bB[d][:], in0=bB[d][:], scalar=0.5, in1=vc[:],
                                                                  op0=ALU.mult, op1=ALU.mult),
                     reads=r_B[d] + [r_vc], writes=r_B[d])
            stage1(c + 1)
            S.op("dve", lambda e: e.tensor_tensor_scan(out=cH[:, 0, :], data0=cC[0][:], data1=cB[0][:],
                                                       initial=0.0, op0=ALU.mult, op1=ALU.add),
                 reads=[r_cC[0], r_cB[0]], writes=[r_cH[0]])
            S.op("dve", lambda e: e.tensor_tensor_scan(
                out=bH[0][:], data0=bC[0][:], data1=bB[0][:], initial=cH[:, 0, CTX - 1:CTX], op0=ALU.mult, op1=ALU.add),
                reads=[r_C[0]] + r_B[0] + [r_cH[0]], writes=[r_H[0]])
            S.op("dve", lambda e: e.tensor_tensor_scan(out=rev_ap(cH[:, 1, :]), data0=rev_ap(cC[1][:]),
                                                       data1=rev_ap(cB[1][:]), initial=0.0,
                                                       op0=ALU.mult, op1=ALU.add),
                 reads=[r_cC[1], r_cB[1]], writes=[r_cH[1]])
            S.op("dve", lambda e: e.tensor_tensor_scan(
                out=rev_ap(bH[1][:]), data0=rev_ap(bC[1][:]), data1=rev_ap(bB[1][:]), initial=cH[:, 1, 0:1],
                op0=ALU.mult, op1=ALU.add), reads=[r_C[1]] + r_B[1] + [r_cH[1]], writes=[r_H[1]])
            S.op("dve", lambda e: e.tensor_tensor(out=bH[0][:], in0=bH[0][:], in1=bH[1][:], op=ALU.add),
                 reads=r_H, writes=[r_H[0]])
            zt, rz, zdn = zb.next()
            S.op("dve", lambda e, zt=zt: e.tensor_tensor(out=zt[:], in0=bH[0][:], in1=bG[p][:], op=ALU.mult),
                 reads=[r_H[0], r_G[p]], writes=[rz])
            S.dma("sp", zdn, lambda e, zt=zt, c=c: e.dma_start(out=zs[c], in_=zt[:]), reads=[rz], writes=[r_zs[c]])
        S.barrier()
        S.reset(m0)
        S.sb_top = top0
        posts = mixer_out(0, recout, True, dst=xs, r_dst=r_xs, next_pre=next_pre)
        S.reset(m0)
        return posts

    def fourier(carry=None, next_pre=None):
        l = 1
        S.soft_switch()
        m0 = S.mark()
        top0 = S.sb_top
        hx = S.sbuf_top("hxf", [128, KC, SEQ], BF16)
        r_hx = [Res(f"hxf{k}") for k in range(KC)]
        W = mixer_pre_work()
        r_hxt = [[Res(f"hxf{t}_{k}") for k in range(KC)] for t in range(NT)]
        pre_pipeline(W, [(xs, r_xs[t], t * TT, TT, l, 1, 0,
                          (lambda kc, t=t: hx[:, kc, t * TT:(t + 1) * TT]), (lambda kc, t=t: r_hxt[t][kc]))
                         for t in range(NT)], carry=carry)
        S.barrier()
        S.reset(M_BASE)
        cs = S.sbuf("cs", [128, 2 * 512], BF16)
        r_cs = Res()
        S.dma("sp", "cs", lambda e: e.dma_start(out=cs[:], in_=cs_d), writes=[r_cs])
        xcs = Ring(S, "xcs", 4, [128, KC, 512], BF16)
        dbuf = Ring(S, "dbuf", 4, [128, 16 * 512], BF16)
        pct = preconv_tasks(1, 1) if nsub >= 6 else []
        n_pct = len(pct)
        for gp in range(4):
            xbs = []
            for g in (2 * gp, 2 * gp + 1):
                xb, rxb, _ = xcs.next()
                xbs.append((g, xb, rxb))
                for n in range(KC):
                    pb = n % 2
                    S.op("pe", lambda e, n=n, pb=pb, g=g: mm_group(e, bank(PA, pb), [
                        (hx[:, 2 * g + jj, n * 128:(n + 1) * 128], cs[:, jj * 512:(jj + 1) * 512])
                        for jj in range(2)]),
                        reads=[r_hx[2 * g], r_hx[2 * g + 1], r_cs], writes=[rPA[pb]])
                    S.op("act" if n % 2 == 0 else "dve",
                         (lambda e, n=n, pb=pb, xb=xb: e.activation(out=xb[:, n, :], in_=bank(PA, pb), func=AF.Copy))
                         if n % 2 == 0 else
                         (lambda e, n=n, pb=pb, xb=xb: e.tensor_copy(out=xb[:, n, :], in_=bank(PA, pb))),
                         reads=[rPA[pb]], writes=[rxb])
            for kt in range(NT):
                dc, rdc, dnc = dbuf.next()
                S.dma("pool", dnc, lambda e, dc=dc, kt=kt: e.dma_start(out=dc[:], in_=dftn[kt, 0]), writes=[rdc])
                ds_, rds, dns = dbuf.next()
                S.dma("pool", dns, lambda e, ds_=ds_, kt=kt: e.dma_start(out=ds_[:], in_=dftn[kt, 1]), writes=[rds])
                while len(pct) > n_pct - (n_pct * (gp * NT + kt + 1)) // (4 * NT):
                    pct.pop(0)()
                i2 = 0
                for (g, xb, rxb) in xbs:
                    for ff in range(2):
                        pb = 2 + i2 % 2
                        i2 += 1
                        S.op("pe", lambda e, ff=ff, pb=pb, xb=xb, dc=dc, ds_=ds_: mm_group(
                            e, bank(PA, pb),
                            [(xb[:, n, ff * 128:(ff + 1) * 128], dc[:, n * 512:(n + 1) * 512]) for n in range(KC)] +
                            [(xb[:, n, 256 + ff * 128:256 + (ff + 1) * 128], ds_[:, n * 512:(n + 1) * 512])
                             for n in range(KC)]),
                            reads=[rxb, rdc, rds], writes=[rPA[pb]])
                        S.op("act", lambda e, ff=ff, pb=pb, kt=kt, g=g: e.activation(
                            out=hx[:, 2 * g + ff, kt * TT:(kt + 1) * TT], in_=bank(PA, pb), func=AF.Copy),
                            reads=[rPA[pb]], writes=[r_hx[2 * g + ff]])
        S.barrier()
        S.reset(m0)
        posts = mixer_out(1, fouout, False, hx=hx, r_hx=r_hx, dst=xs, r_dst=r_xs, next_pre=next_pre)
        S.reset(m0)
        S.sb_top = top0
        return posts

    def xjobs(src, r_src, dst, r_dst):
        return [(src, r_src, dst, r_dst, t * TT, TT, t, 0) for t in range(NT)]

    adaln_start()
    final = lambda k: (outT, r_out) if nsub == k else (xs, r_xs)
    d_, rd_ = final(1)
    carry = ffn(0, 0, xjobs(xT, r_xT, d_, rd_) + [(ctxT, r_ctxT, ctxs, r_ctxs, 0, CTX, 0, 1)],
                bg_l=0, bg_idxs=range(48, 144), defer=True)
    adaln_evac(0, 48, 144)
    der_compute(0, 1)
    der_compute(0, 2)
    if nsub >= 2:
        d3, rd3 = final(3)
        j3 = xjobs(xs, r_xs, d3, rd3)
        carry = rglru(carry, next_pre=(lambda: ffn_first_pre(0, 1, j3[0])) if nsub >= 3 else None)
    if nsub >= 3:
        d4, rd4 = final(4)
        j4 = xjobs(xs, r_xs, d4, rd4)

        def l1_mods():
            adaln_evac(1, 0, 144)
            for s_ in range(3):
                der_compute(1, s_)
        carry = ffn(0, 1, j3, bg_l=1, bg_idxs=range(0, 144), carry=carry, defer=True, pre_done=True,
                    before_last_y=l1_mods, next_pre=(lambda: ffn_first_pre(1, 0, j4[0])) if nsub >= 4 else None)
    if nsub >= 4:
        carry = ffn(1, 0, j4, carry=carry, defer=True, pre_done=True)
    if nsub >= 5:
        d6, rd6 = final(6)
        j6 = xjobs(xs, r_xs, d6, rd6)
        carry = fourier(carry, next_pre=(lambda: ffn_first_pre(1, 1, j6[0])) if nsub >= 6 else None)
    if nsub >= 6:
        carry = ffn(1, 1, j6, carry=carry, defer=True, pre_done=True)
    for st in carry:
        st()
    S.barrier()
    if nsub in (2, 5):
        m0 = S.mark()
        cp = Ring(S, "cp", 2, [128, SEQ], F32)
        for m in range(KC):
            b, rb, dn = cp.next()
            S.dma("sp", dn, lambda e, b=b, m=m: e.dma_start(out=b[:], in_=xs[m]),
                  reads=[r_xs[t][m] for t in range(NT)], writes=[rb])
            S.dma("sp", dn + "o", lambda e, b=b, m=m: e.dma_start(out=outT[m], in_=b[:]), reads=[rb],
                  writes=[r_out[t][m] for t in range(NT)])
        S.reset(m0)
    S.barrier()
    S.close()
    return nc


def _fm(v):
    v = np.asarray(v, np.float32)
    lead = v.shape[:-1]
    return np.moveaxis(v.reshape(lead + (KC, 128)), -1, 0)


def prep_shared(inp):
    f32 = np.float32
    mod_w = np.asarray(inp["mod_w"], f32)
    modw_t = np.ascontiguousarray(mod_w.reshape(2, KC, 128, 144, 128).transpose(0, 3, 2, 1, 4)).reshape(2, 144, 128, 16 * 128)
    w_in = np.asarray(inp["ffn_w_in"], f32)
    win_t = np.ascontiguousarray(w_in.reshape(2, 2, KC, 128, 2, JC, 128).transpose(0, 1, 5, 3, 2, 4, 6)).reshape(2, 2, JC, 128, 16 * 256)
    w_out = np.asarray(inp["ffn_w_out"], f32)
    wout_t = np.ascontiguousarray(w_out.reshape(2, 2, JC, 128, KC, 128).transpose(0, 1, 4, 3, 2, 5)).reshape(2, 2, KC, 128, JC * 128)
    r_in = np.asarray(inp["rec_w_in"], f32)[0]
    recin_t = np.ascontiguousarray(r_in.reshape(KC, 128, 2, KC, 128).transpose(3, 1, 0, 2, 4)).reshape(KC, 128, 16 * 256)
    r_out = np.asarray(inp["rec_w_out"], f32)[0]
    recout_t = np.ascontiguousarray(r_out.reshape(KC, 128, KC, 128).transpose(2, 1, 0, 3)).reshape(KC, 128, 16 * 128)
    f_out = np.asarray(inp["fou_w_out"], f32)[0]
    fouout_t = np.ascontiguousarray(f_out.reshape(KC, 128, KC, 128).transpose(2, 1, 0, 3)).reshape(KC, 128, 16 * 128)
    g_w = np.asarray(inp["rec_gate_w"], f32)[0]
    gw_t = np.ascontiguousarray(g_w.transpose(2, 3, 0, 1, 4)).reshape(KC, 128, 4 * 128)
    j = np.arange(256)
    ang = 2 * np.pi * ((j[:, None] * j[None, :]) % 256) / 256.0
    cc, sc = np.cos(ang) / 16.0, np.sin(ang) / 16.0
    cs = np.concatenate([cc, sc], axis=1).reshape(2, 128, 512).transpose(1, 0, 2).reshape(128, 1024)
    n = np.arange(SEQ)
    angn = 2 * np.pi * ((n[:, None] * n[None, :]) % SEQ) / float(SEQ)
    sq = 1.0 / math.sqrt(SEQ)
    cn, sn = np.cos(angn) * sq, -np.sin(angn) * sq
    dft = np.stack([cn, sn], 0).reshape(2, KC, 128, NT, 512).transpose(3, 0, 2, 1, 4).reshape(NT, 2, 128, 16 * 512)
    bf = ml_dtypes.bfloat16
    return dict(modw=modw_t, win=win_t, wout=wout_t, recin=recin_t, recout=recout_t, fouout=fouout_t, gw=gw_t,
                cs=np.ascontiguousarray(cs).astype(bf), dftn=np.ascontiguousarray(dft).astype(bf))


def prep_core(inp, b):
    f32 = np.float32
    x = np.asarray(inp["x"], f32)[b]
    ctx = np.asarray(inp["ctx"], f32)[b]
    vecs = np.zeros((128, NV), f32)
    cc = np.stack([np.asarray(inp["c"], f32)[b], np.asarray(inp["c_ctx"], f32)], 0)
    vecs[:, V_C:V_C + 32] = _fm(cc).transpose(0, 2, 1).reshape(128, 32)
    vecs[:, V_MODB:V_MODB + 288] = _fm(np.asarray(inp["mod_b"], f32).reshape(2, 9, D)).reshape(128, 288)
    vecs[:, V_NG:V_NG + 192] = _fm(np.asarray(inp["norm_g"], f32)).reshape(128, 192)
    vecs[:, V_GB:V_GB + 64] = _fm(np.asarray(inp["rec_gate_b"], f32)[0]).reshape(128, 64)
    vecs[:, V_LAM:V_LAM + 32] = _fm(np.asarray(inp["rec_lam"], f32)[0]).reshape(128, 32)
    vecs[:, V_CW:V_CW + 64] = _fm(np.asarray(inp["rec_conv_w"], f32)[0]).reshape(128, 64)
    vecs[:, V_CB:V_CB + 16] = _fm(np.asarray(inp["rec_conv_b"], f32)[0]).reshape(128, 16)
    return dict(xT=np.ascontiguousarray(x.T).reshape(KC, 128, SEQ),
                ctxT=np.ascontiguousarray(ctx.T).reshape(KC, 128, CTX), vecs=vecs)


_NC_CACHE = {}


def run(inputs, nsub=6, cores=8):
    if nsub not in _NC_CACHE:
        _NC_CACHE[nsub] = build(nsub)
    nc = _NC_CACHE[nsub]
    import time as _t
    _t0 = _t.time()
    shared = prep_shared(inputs)
    in_maps = []
    for b in range(cores):
        m = dict(shared)
        m.update(prep_core(inputs, b))
        in_maps.append(m)
    _t1 = _t.time()
    res = run_bass_kernel_spmd(nc, in_maps, core_ids=list(range(cores)))
    print(f"[kernel] prep {_t1 - _t0:.1f}s  spmd {_t.time() - _t1:.1f}s", flush=True)
    out = np.stack([np.ascontiguousarray(r["outT"].reshape(D, SEQ).T) for r in res.results], 0)
    return out.astype(np.float32)


def kernel(**inputs):
    return run(inputs, 6, 8)
```

```python
import math
from contextlib import ExitStack
import numpy as np
import ml_dtypes
import concourse.bass as bass
import concourse.mybir as mybir
from concourse.bass_utils import run_bass_kernel_spmd

F32 = mybir.dt.float32
BF16 = mybir.dt.bfloat16
AF = mybir.ActivationFunctionType
ALU = mybir.AluOpType

D = 2048
KC = 16
SEQ = 2048
TT = 512
NT = 4
CTX = 256
DFF = 5632
JC = 44
EPS = 1e-6
NV = 688
V_C, V_MODB, V_NG, V_GB, V_LAM, V_CW, V_CB = 0, 32, 320, 512, 576, 608, 672
SB_BASE = 16384 + 128
SB_LIMIT = 228992


class Res:
    __slots__ = ("name", "w", "r")

    def __init__(self, name=""):
        self.name = name
        self.w = None
        self.r = []


class Sched:
    def __init__(self, nc):
        self.nc = nc
        self.es = ExitStack()
        self.eng = {"pe": nc.tensor, "act": nc.scalar, "dve": nc.vector,
                    "pool": nc.gpsimd, "sp": nc.sync}
        self.sems = {}
        self.cnt = {}
        self.seen = {k: {} for k in self.eng}
        for k in ("pe", "act", "dve", "pool"):
            self._sem("E_" + k)
        self.sb_off = SB_BASE
        self.sb_top = SB_LIMIT
        self.uid = 0

    def _sem(self, key):
        if key not in self.sems:
            self.sems[key] = self.es.enter_context(self.nc.semaphore(key))
            self.cnt[key] = 0
        return self.sems[key]

    def sbuf(self, name, shape, dtype):
        esz = 2 if dtype == BF16 else 4
        nbytes = int(np.prod(shape[1:])) * esz
        nbytes = (nbytes + 63) // 64 * 64
        self.uid += 1
        t = self.nc.alloc_sbuf_tensor_at(f"{name}_{self.uid}", list(shape), dtype, offset=self.sb_off)
        self.sb_off += nbytes
        assert self.sb_off <= self.sb_top, (name, self.sb_off, self.sb_top)
        return t

    def sbuf_top(self, name, shape, dtype):
        esz = 2 if dtype == BF16 else 4
        nbytes = int(np.prod(shape[1:])) * esz
        nbytes = (nbytes + 63) // 64 * 64
        self.uid += 1
        self.sb_top -= nbytes
        assert self.sb_off <= self.sb_top, (name, self.sb_off, self.sb_top)
        return self.nc.alloc_sbuf_tensor_at(f"{name}_{self.uid}", list(shape), dtype, offset=self.sb_top)

    def soft_switch(self):
        for s in ("sp", "pool"):
            for key, c in self.cnt.items():
                if c > 0:
                    self._wait(s, (key, c))

    def mark(self):
        return self.sb_off

    def reset(self, m):
        self.sb_off = m

    def _wait(self, stream, ev):
        if ev is None:
            return
        key, val = ev
        if self.seen[stream].get(key, 0) >= val:
            return
        self.seen[stream][key] = val
        self.eng[stream].wait_ge(self.sems[key], val)

    def _deps(self, stream, reads, writes):
        for r in reads:
            self._wait(stream, r.w)
        for w in writes:
            self._wait(stream, w.w)
            for ev in w.r:
                self._wait(stream, ev)

    def _commit(self, ev, reads, writes):
        for r in reads:
            r.r.append(ev)
        for w in writes:
            w.w = ev
            w.r = []

    def op(self, stream, fn, reads=(), writes=()):
        self._deps(stream, reads, writes)
        ins = fn(self.eng[stream])
        key = "E_" + stream
        self.cnt[key] += 1
        ins.then_inc(self.sems[key], 1)
        ev = (key, self.cnt[key])
        self._commit(ev, reads, writes)
        return ev

    def dma(self, stream, dsem, fn, reads=(), writes=()):
        self._deps(stream, reads, writes)
        key = "D_" + dsem
        sem = self._sem(key)
        inss = fn(self.eng[stream])
        if not isinstance(inss, (list, tuple)):
            inss = [inss]
        for ins in inss:
            ins.then_inc(sem, 16)
            self.cnt[key] += 16
        ev = (key, self.cnt[key])
        self._commit(ev, reads, writes)
        return ev

    def barrier(self):
        for s in self.eng:
            for key, c in self.cnt.items():
                if c > 0:
                    self._wait(s, (key, c))

    def close(self):
        self.es.close()


def rev_ap(ap):
    pat = [list(p) for p in ap.ap]
    assert len(pat) == 2, pat
    step, n = pat[1]
    return bass.AP(ap.tensor, ap.offset + step * (n - 1), [pat[0], [-step, n]])


class Ring:
    def __init__(self, S, name, n, shape, dtype):
        self.name = name
        self.n = n
        self.bufs = [S.sbuf(f"{name}{i}", shape, dtype) for i in range(n)]
        self.res = [Res(f"{name}{i}") for i in range(n)]
        self.i = 0

    def next(self):
        k = self.i % self.n
        self.i += 1
        return self.bufs[k], self.res[k], f"{self.name}{k}"


def mm_group(e, out, pairs):
    n = len(pairs)
    ins = None
    for i, (l, r) in enumerate(pairs):
        ins = e.matmul(out, l, r, start=(i == 0), stop=(i == n - 1))
    return ins


def build(nsub=6):
    nc = bass.Bass("TRN2", target_bir_lowering=False)
    dt_in = lambda name, shape, dt=F32: nc.dram_tensor(name, list(shape), dt, kind="ExternalInput").ap()
    xT = dt_in("xT", [KC, 128, SEQ])
    ctxT = dt_in("ctxT", [KC, 128, CTX])
    vecs_d = dt_in("vecs", [128, NV])
    modw = dt_in("modw", [2, 144, 128, 16 * 128])
    win = dt_in("win", [2, 2, JC, 128, 16 * 256])
    wout = dt_in("wout", [2, 2, KC, 128, JC * 128])
    recin = dt_in("recin", [KC, 128, 16 * 256])
    recout = dt_in("recout", [KC, 128, 16 * 128])
    fouout = dt_in("fouout", [KC, 128, 16 * 128])
    gw = dt_in("gw", [KC, 128, 4 * 128])
    cs_d = dt_in("cs", [128, 2 * 512], BF16)
    dftn = dt_in("dftn", [NT, 2, 128, 16 * 512], BF16)
    outT = nc.dram_tensor("outT", [KC, 128, SEQ], F32, kind="ExternalOutput").ap()
    xs = nc.dram_tensor("xs", [KC, 128, SEQ], F32, kind="Internal").ap()
    ctxs = nc.dram_tensor("ctxs", [KC, 128, CTX], F32, kind="Internal").ap()
    zs = nc.dram_tensor("zs", [KC, 128, SEQ], BF16, kind="Internal").ap()
    winc = nc.dram_tensor("winc", [2, 2, JC, 128, 16 * 256], BF16, kind="Internal").ap()
    woutc = nc.dram_tensor("woutc", [2, 2, KC, 128, JC * 128], BF16, kind="Internal").ap()
    mixc = nc.dram_tensor("mixc", [2, KC, 128, 16 * 128], BF16, kind="Internal").ap()

    S = Sched(nc)
    PA = S.es.enter_context(nc.psum_tensor("PA", [128, 2048], F32))
    PB = S.es.enter_context(nc.psum_tensor("PB", [128, 2048], F32))
    rPA = [Res(f"pa{i}") for i in range(4)]
    rPB = [Res(f"pb{i}") for i in range(4)]
    bank = lambda P, b, n=512: P[:, b * 512:b * 512 + n]

    def dres(name, ntile):
        return [[Res(f"{name}{t}_{m}") for m in range(KC)] for t in range(ntile)]
    r_xT, r_xs, r_out = dres("xT", NT), dres("xs", NT), dres("out", NT)
    r_ctxT, r_ctxs = dres("ctxT", 1), dres("ctxs", 1)
    r_zs = [Res(f"zs{c}") for c in range(KC)]

    vecs = S.sbuf("vecs", [128, NV], F32)
    mods = [S.sbuf(f"mods{l}", [128, 288], F32) for l in range(2)]
    der = S.sbuf("der", [128, 384], F32)
    ones = S.sbuf("ones", [128, 128], BF16)
    epsb = S.sbuf("epsb", [128, 1], F32)
    oneb = S.sbuf("oneb", [128, 1], F32)
    scb = S.sbuf("scb", [128, 32], BF16)
    coef = S.sbuf("coef", [128, 64], F32)
    r_vecs, r_mods, r_der, r_const, r_scb, r_coef = Res(), [Res(), Res()], Res(), Res(), Res(), Res()
    r_modps = [Res(f"modps{i}") for i in range(144)]

    S.dma("sp", "vecs", lambda e: e.dma_start(out=vecs[:], in_=vecs_d), writes=[r_vecs])
    S.op("dve", lambda e: e.memset(ones[:], 1.0), writes=[r_const])
    S.op("dve", lambda e: e.memset(epsb[:], EPS), writes=[r_const])
    S.op("dve", lambda e: e.memset(oneb[:], 1.0), writes=[r_const])

    def vcol(base, idx):
        return vecs[:, base + idx:base + idx + 1]

    def modcol(l, q, kc, which):
        i = (q * 16 + kc) * 2 + which
        return mods[l][:, i:i + 1]

    def modvec(l, q, which):
        t = mods[l]
        a = t[:, 0:1]
        return bass.AP(a.tensor, a.offset + q * 32 + which, [list(a.ap[0]), [2, 16]])

    def dercol(l, s, which, kind, kc):
        i = ((l * 3 + s) * 2 + which) * 32 + kind * 16 + kc
        return der[:, i:i + 1]

    def adaln_load(l, idx, wm):
        buf, rb, dn = wm.next()
        S.dma("pool", dn, lambda e: e.dma_start(out=buf[:], in_=modw[l, idx]), writes=[rb])
        return buf, rb

    def adaln_mm(idx, buf, rb):
        S.op("pe", lambda e: mm_group(e, PB[:, 1536 + idx * 2:1536 + idx * 2 + 2],
                                      [(buf[:, k * 128:(k + 1) * 128], scb[:, 2 * k:2 * k + 2]) for k in range(KC)]),
             reads=[rb, r_scb], writes=[r_modps[idx]])

    def adaln_tasks(l, idxs, wm, lag=2):
        pend = []
        tasks = []

        def mk_load(idx):
            def f():
                pend.append((idx,) + adaln_load(l, idx, wm))
            return f

        def mk_mm():
            def f():
                idx, buf, rb = pend.pop(0)
                adaln_mm(idx, buf, rb)
            return f
        seq = list(idxs)
        for i, idx in enumerate(seq):
            def both(idx=idx, i=i):
                mk_load(idx)()
                if i >= lag:
                    mk_mm()()
            tasks.append(both)
        for _ in range(min(lag, len(seq))):
            tasks.append(mk_mm())
        return tasks

    def adaln_evac(l, a, b):
        n = b - a
        for which in range(2):
            def ev(e, which=which):
                t = mods[l][:, 0:1]
                o = bass.AP(t.tensor, t.offset + 2 * a + which, [list(t.ap[0]), [2, n]])
                p = PB[:, 1536:1537]
                pi = bass.AP(p.tensor, p.offset + 2 * a + which, [list(p.ap[0]), [2, n]])
                return e.tensor_tensor(out=o, in0=pi, in1=vecs[:, V_MODB + l * 144 + a:V_MODB + l * 144 + b],
                                       op=ALU.add)
            S.op("dve", ev, reads=r_modps[a:b] + [r_vecs], writes=[r_mods[l]])

    def der_compute(l, s):
        q0 = 3 * s
        gc = 0.5 if s != 1 else 1.0
        for which in range(2):
            def da(e, which=which):
                i0 = ((l * 3 + s) * 2 + which) * 32
                return e.scalar_tensor_tensor(
                    out=der[:, i0:i0 + 16], in0=modvec(l, q0 + 1, which), scalar=1.0,
                    in1=vecs[:, V_NG + (l * 6 + 2 * s) * 16:V_NG + (l * 6 + 2 * s) * 16 + 16],
                    op0=ALU.add, op1=ALU.mult)
            S.op("dve", da, reads=[r_mods[l], r_vecs], writes=[r_der])

            def dg(e, which=which):
                i0 = ((l * 3 + s) * 2 + which) * 32 + 16
                return e.scalar_tensor_tensor(
                    out=der[:, i0:i0 + 16], in0=modvec(l, q0 + 2, which), scalar=gc,
                    in1=vecs[:, V_NG + (l * 6 + 2 * s + 1) * 16:V_NG + (l * 6 + 2 * s + 1) * 16 + 16],
                    op0=ALU.mult, op1=ALU.mult)
            S.op("dve", dg, reads=[r_mods[l], r_vecs], writes=[r_der])

    def adaln_start():
        S.op("act", lambda e: e.activation(out=scb[:], in_=vecs[:, V_C:V_C + 32], func=AF.Silu),
             reads=[r_vecs], writes=[r_scb])
        m0 = S.mark()
        wm = Ring(S, "wm", 6, [128, 16 * 128], BF16)
        for t in adaln_tasks(0, range(0, 48), wm, lag=4):
            t()
        adaln_evac(0, 0, 48)
        der_compute(0, 0)
        S.barrier()
        S.reset(m0)

    class Work:
        pass

    def alloc_common(W, n_xinA=3):
        W.xy = S.sbuf("xy", [128, KC, TT], F32)
        W.r_xy = [Res(f"xy{m}") for m in range(KC)]
        W.rs = S.sbuf("rs", [128, TT], F32)
        W.r_rs = Res("rs")
        W.rsq = S.sbuf("rsq", [128, TT], F32)
        W.r_rsq = Res("rsq")
        W.tmp = Ring(S, "tmp", 2, [128, TT], F32)
        W.sq2 = Ring(S, "sq2", 2, [128, TT], BF16)
        W.xin = Ring(S, "xin", 3, [128, TT], F32)
        W.xinA = Ring(S, "xinA", n_xinA, [128, TT], F32)
        W.hb = S.sbuf("hb", [128, KC, TT], BF16)
        W.r_hb = [Res(f"hb{k}") for k in range(KC)]
        W.pre_alt = 0
        if n_xinA > 3:
            W.rs2 = S.sbuf("rs2", [128, TT], F32)
            W.r_rs2 = Res("rs2")

    M_BASE = S.mark()
    WC = Work()
    alloc_common(WC)
    P_MARK = S.mark()

    def rstd_from_stat(W, T):
        S.op("act", lambda e: e.activation(out=W.rs[:, :T], in_=bank(PB, 2, T), func=AF.Sqrt,
                                           bias=epsb[:], scale=1.0 / D),
             reads=[rPB[2], r_const], writes=[W.r_rs])
        S.op("dve", lambda e: e.reciprocal(out=W.rs[:, :T], in_=W.rs[:, :T]), reads=[W.r_rs], writes=[W.r_rs])

    def pre_phase(W, src, r_src, t0, T, l, s, which, dst, r_dst):
        S.dma("sp", "xy", lambda e: e.dma_start(out=W.xy[:, :, :T],
                                                in_=src[:, :, t0:t0 + T].rearrange("c p t -> p c t")),
              reads=r_src, writes=W.r_xy)
        for kc in range(KC):
            S.op("act", lambda e, kc=kc: e.activation(out=dst(kc), in_=W.xy[:, kc, :T], func=AF.Square),
                 reads=[W.r_xy[kc]], writes=[r_dst(kc)])
        S.op("pe", lambda e: mm_group(e, bank(PB, 2, T), [(ones[:], dst(kc)) for kc in range(KC)]),
             reads=[r_const] + [r_dst(kc) for kc in range(KC)], writes=[rPB[2]])
        rstd_from_stat(W, T)
        for kc in range(KC):
            tb, rt, _ = W.tmp.next()
            S.op("dve", lambda e, kc=kc, tb=tb: e.tensor_tensor(out=tb[:, :T], in0=W.xy[:, kc, :T],
                                                                 in1=W.rs[:, :T], op=ALU.mult),
                 reads=[W.r_xy[kc], W.r_rs], writes=[rt])
            S.op("act", lambda e, kc=kc, tb=tb: e.activation(
                out=dst(kc), in_=tb[:, :T], func=AF.Identity,
                bias=modcol(l, 3 * s, kc, which), scale=dercol(l, s, which, 0, kc)),
                reads=[rt, r_der, r_mods[l]], writes=[r_dst(kc)])

    def y_phase(W, wtile, nk, rhs, r_rhs, T, wo, cache=None, inter=None):
        pend = None
        inter = list(inter) if inter else []
        n_inter = len(inter)
        for m in range(KC):
            want_left = n_inter - (n_inter * (m + 1)) // KC
            while len(inter) > want_left:
                inter.pop(0)()
            buf, rb, dn = wo.next()
            if cache is not None and not cache[2]:
                S.dma("pool", dn, lambda e, buf=buf, m=m: e.dma_start(out=buf[:, :nk * 128], in_=wtile(m)),
                      reads=[cache[1][m]], writes=[rb])
            else:
                S.dma("pool", dn, lambda e, buf=buf, m=m: e.dma_start(
                    out=buf[:, :nk * 128], in_=wtile(m), max_dma_last_dim=8192), writes=[rb])
                if cache is not None:
                    S.dma("sp", dn + "s", lambda e, buf=buf, m=m: e.dma_start(out=cache[0](m), in_=buf[:, :nk * 128]),
                          reads=[rb], writes=[cache[1][m]])
            pb = m % 2
            S.op("pe", lambda e, buf=buf, pb=pb: mm_group(
                e, bank(PB, pb, T), [(buf[:, k * 128:(k + 1) * 128], rhs(k)) for k in range(nk)]),
                reads=[rb] + r_rhs, writes=[rPB[pb]])
            if pend is not None:
                pend()
            S.op("act", lambda e, m=m, pb=pb: e.activation(out=W.xy[:, m, :T], in_=bank(PB, pb, T), func=AF.Copy),
                 reads=[rPB[pb]], writes=[W.r_xy[m]])
            sb, rsq, _ = W.sq2.next()
            S.op("act", lambda e, sb=sb, pb=pb: e.activation(out=sb[:, :T], in_=bank(PB, pb, T), func=AF.Square),
                 reads=[rPB[pb]], writes=[rsq])

            def stat(m=m, sb=sb, rsq=rsq):
                S.op("pe", lambda e: e.matmul(bank(PB, 2, T), ones[:], sb[:, :T], start=(m == 0), stop=(m == KC - 1)),
                     reads=[rsq, r_const], writes=[rPB[2]])
            pend = stat
        pend()

    def post_steps(W, src, r_src, dst, r_dstd, t0, T, l, s, which):
        loads = {}
        dpp = W.xin.n - 3 if W.xin.n >= 6 else W.xin.n - 1

        def load(m):
            xb, rx, dn = W.xin.next()
            S.dma("sp", dn, lambda e: e.dma_start(out=xb[:, :T], in_=src[m, :, t0:t0 + T]),
                  reads=[r_src[m]], writes=[rx])
            loads[m] = (xb, rx, dn)

        def step_r():
            S.op("act", lambda e: e.activation(out=W.rsq[:, :T], in_=bank(PB, 2, T), func=AF.Sqrt,
                                               bias=epsb[:], scale=1.0 / D),
                 reads=[rPB[2], r_const], writes=[W.r_rsq])
            S.op("dve", lambda e: e.reciprocal(out=W.rsq[:, :T], in_=W.rsq[:, :T]), reads=[W.r_rsq],
                 writes=[W.r_rsq])
            for i in range(dpp):
                load(i)

        def mk(m):
            def f():
                if m + dpp < KC:
                    load(m + dpp)
                xb, rx, dn = loads.pop(m)
                S.op("dve", lambda e: e.tensor_tensor(out=W.xy[:, m, :T], in0=W.xy[:, m, :T], in1=W.rsq[:, :T],
                                                      op=ALU.mult),
                     reads=[W.r_rsq], writes=[W.r_xy[m]])
                S.op("dve", lambda e: e.scalar_tensor_tensor(
                    out=xb[:, :T], in0=W.xy[:, m, :T], scalar=dercol(l, s, which, 1, m), in1=xb[:, :T],
                    op0=ALU.mult, op1=ALU.add), reads=[W.r_xy[m], r_der], writes=[rx])
                S.dma("sp", dn + "s", lambda e: e.dma_start(out=dst[m, :, t0:t0 + T], in_=xb[:, :T]),
                      reads=[rx], writes=[r_dstd[m]])
            return f
        return [step_r] + [mk(m) for m in range(KC)]

    def post_phase(W, src, r_src, dst, r_dstd, t0, T, l, s, which):
        for st in post_steps(W, src, r_src, dst, r_dstd, t0, T, l, s, which):
            st()

    def pre_steps(W, src, r_src, t0, T, l, s, which, dst, r_dst, alt=0):
        loads = {}
        dp = 3 if W.xinA.n >= 10 else 2
        sb_ = alt
        rs_, r_rs_ = (W.rs, W.r_rs) if alt == 0 else (W.rs2, W.r_rs2)

        def load(tag, kc):
            xb, rx, dn = W.xinA.next()
            S.dma("sp", dn, lambda e: e.dma_start(out=xb[:, :T], in_=src[kc, :, t0:t0 + T]),
                  reads=[r_src[kc]], writes=[rx])
            loads[(tag, kc)] = (xb, rx)

        def mkA(kc):
            def f():
                if kc == 0:
                    for i in range(dp):
                        load("A", i)
                if kc + dp < KC:
                    load("A", kc + dp)
                xb, rx = loads.pop(("A", kc))
                S.op("act", lambda e: e.activation(out=dst(kc), in_=xb[:, :T], func=AF.Square),
                     reads=[rx], writes=[r_dst(kc)])
                S.op("pe", lambda e: e.matmul(bank(PA, sb_, T), ones[:], dst(kc), start=(kc == 0), stop=(kc == KC - 1)),
                     reads=[r_const, r_dst(kc)], writes=[rPA[sb_]])
            return f

        def step_r():
            for i in range(dp):
                load("B", i)
            S.op("act", lambda e: e.activation(out=rs_[:, :T], in_=bank(PA, sb_, T), func=AF.Sqrt,
                                               bias=epsb[:], scale=1.0 / D),
                 reads=[rPA[sb_], r_const], writes=[r_rs_])
            S.op("dve", lambda e: e.reciprocal(out=rs_[:, :T], in_=rs_[:, :T]), reads=[r_rs_], writes=[r_rs_])

        def mkB(kc):
            def f():
                if kc + dp < KC:
                    load("B", kc + dp)
                xb, rx = loads.pop(("B", kc))
                S.op("dve", lambda e: e.tensor_tensor(out=xb[:, :T], in0=xb[:, :T], in1=rs_[:, :T], op=ALU.mult),
                     reads=[r_rs_], writes=[rx])
                S.op("act", lambda e: e.activation(
                    out=dst(kc), in_=xb[:, :T], func=AF.Identity,
                    bias=modcol(l, 3 * s, kc, which), scale=dercol(l, s, which, 0, kc)),
                    reads=[rx, r_der, r_mods[l]], writes=[r_dst(kc)])
            return f
        return [mkA(kc) for kc in range(KC)] + [step_r] + [mkB(kc) for kc in range(KC)]

    def pre_pipeline(W, jobs_args, carry=None):
        lists = [pre_steps(W, *a, alt=i % 2) for i, a in enumerate(jobs_args)]
        carry = list(carry) if carry else []
        for st in lists[0][:KC]:
            st()
            if carry:
                carry.pop(0)()
        for i, L in enumerate(lists):
            L[KC]()
            nxt = lists[i + 1][:KC] if i + 1 < len(lists) else []
            for k in range(KC):
                L[KC + 1 + k]()
                if carry:
                    carry.pop(0)()
                if nxt:
                    nxt[k]()
        for st in carry:
            st()

    cache_res = {}

    def get_cache_res(l, f):
        if (l, f) not in cache_res:
            cache_res[(l, f)] = dict(win=[Res(f"winc{l}{f}_{j}") for j in range(JC)],
                                     wout=[Res(f"woutc{l}{f}_{m}") for m in range(KC)], pre=False)
        return cache_res[(l, f)]

    pc_slots = [Res(f"pcslot{i}") for i in range(8)]
    pc_i = [0]

    def preconv_tasks(l, f):
        cr = get_cache_res(l, f)
        cr["pre"] = True
        tasks = []

        def mk(src, dstc, r):
            def t():
                k = pc_i[0] % len(pc_slots)
                pc_i[0] += 1
                S.dma("pool", f"pc{k}", lambda e: e.dma_start(out=dstc, in_=src, max_dma_last_dim=8192),
                      writes=[r, pc_slots[k]])
            return t
        for j in range(JC):
            tasks.append(mk(win[l, f, j], winc[l, f, j], cr["win"][j]))
        for m in range(KC):
            tasks.append(mk(wout[l, f, m], woutc[l, f, m], cr["wout"][m]))
        return tasks

    def ffn_first_pre(l, f, job):
        (src, r_src, dst, r_dd, t0, T, ti, which) = job
        s_ = 0 if f == 0 else 2
        return pre_steps(WC, src, r_src[ti], t0, T, l, s_, which,
                         lambda kc, T=T: WC.hb[:, kc, :T], lambda kc: WC.r_hb[kc])

    def ffn(l, f, jobs, bg_l=None, bg_idxs=(), carry=None, defer=False, pre_done=False, before_last_y=None,
            next_pre=None):
        s = 0 if f == 0 else 2
        S.soft_switch()
        m0 = S.mark()
        W = WC
        wi = Ring(S, "wi", 5, [128, 16 * 256], BF16)
        wo = Ring(S, "wo", 3, [128, JC * 128], BF16)
        wm = Ring(S, "wm", 3, [128, 16 * 128], BF16)
        bg = adaln_tasks(bg_l, bg_idxs, wm, lag=2) if bg_l is not None else []
        n_bg_iters = max(1, (len(jobs) - 1) * JC)
        bg_done = 0
        bg_iter = 0
        cr = get_cache_res(l, f)
        r_winc, r_woutc, pre_conv = cr["win"], cr["wout"], cr["pre"]
        hb, r_hb = W.hb, W.r_hb
        hid = S.sbuf("hid", [128, JC, TT], BF16)
        r_hid = [Res(f"hid{j}") for j in range(JC)]
        def mk_pre(job):
            (src, r_src, dst, r_dd, t0, T, ti, which) = job
            return pre_steps(W, src, r_src[ti], t0, T, l, s, which,
                             lambda kc, T=T: hb[:, kc, :T], lambda kc: r_hb[kc])

        def mk_post(job):
            (src, r_src, dst, r_dd, t0, T, ti, which) = job
            return post_steps(W, src, r_src[ti], dst, r_dd[ti], t0, T, l, s, which)
        if not pre_done:
            for st in mk_pre(jobs[0]):
                st()
        posts = list(carry) if carry else []
        for ji, (src, r_src, dst, r_dd, t0, T, ti, which) in enumerate(jobs):
            n_post = len(posts)
            for j in range(JC):
                want_left = n_post - (n_post * (j + 1)) // JC
                while len(posts) > want_left:
                    posts.pop(0)()
                buf, rb, dn = wi.next()
                if ji == 0 and not pre_conv:
                    S.dma("pool", dn, lambda e, buf=buf, j=j: e.dma_start(
                        out=buf[:], in_=win[l, f, j], max_dma_last_dim=8192), writes=[rb])
                    S.dma("sp", dn + "s", lambda e, buf=buf, j=j: e.dma_start(out=winc[l, f, j], in_=buf[:]),
                          reads=[rb], writes=[r_winc[j]])
                else:
                    S.dma("pool", dn, lambda e, buf=buf, j=j: e.dma_start(out=buf[:], in_=winc[l, f, j]),
                          reads=[r_winc[j]], writes=[rb])
                    if ji > 0:
                        bg_iter += 1
                        want = (len(bg) + bg_done) * bg_iter // n_bg_iters
                        while bg and bg_done < want:
                            bg.pop(0)()
                            bg_done += 1
                pb = j % 2

                def pe(e, buf=buf, pb=pb):
                    mm_group(e, bank(PA, pb, T), [(buf[:, k * 256:k * 256 + 128], hb[:, k, :T]) for k in range(KC)])
                    return mm_group(e, bank(PA, 2 + pb, T),
                                    [(buf[:, k * 256 + 128:k * 256 + 256], hb[:, k, :T]) for k in range(KC)])
                S.op("pe", pe, reads=[rb] + r_hb, writes=[rPA[pb], rPA[2 + pb]])
                tb, rt, _ = W.tmp.next()
                S.op("act", lambda e, tb=tb, pb=pb: e.activation(out=tb[:, :T], in_=bank(PA, pb, T), func=AF.Silu),
                     reads=[rPA[pb]], writes=[rt])
                S.op("dve", lambda e, tb=tb, pb=pb, j=j: e.tensor_tensor(
                    out=hid[:, j, :T], in0=tb[:, :T], in1=bank(PA, 2 + pb, T), op=ALU.mult),
                    reads=[rt, rPA[2 + pb]], writes=[r_hid[j]])
            if ji + 1 < len(jobs):
                inter = mk_pre(jobs[ji + 1])
            else:
                while bg:
                    bg.pop(0)()
                if before_last_y is not None:
                    before_last_y()
                inter = next_pre() if next_pre is not None else None
            if ji == 0 and not pre_conv:
                y_phase(W, lambda m: wout[l, f, m], JC, lambda k: hid[:, k, :T], r_hid, T, wo,
                        cache=(lambda m: woutc[l, f, m], r_woutc, True), inter=inter)
            else:
                y_phase(W, lambda m: woutc[l, f, m], JC, lambda k: hid[:, k, :T], r_hid, T, wo,
                        cache=(None, r_woutc, False), inter=inter)
            posts = mk_post(jobs[ji])
        while bg:
            bg.pop(0)()
        S.reset(m0)
        if defer:
            return posts
        for st in posts:
            st()
        S.barrier()
        return None

    def mixer_out(l, wsrc, from_zs, hx=None, r_hx=None, dst=None, r_dst=None, next_pre=None):
        W = WC
        W2 = Work()
        W2.__dict__.update(W.__dict__)
        W2.xy = S.sbuf("xyb", [128, KC, TT], F32)
        W2.r_xy = [Res(f"xyb{m}") for m in range(KC)]
        W2.rsq = S.sbuf("rsqb", [128, TT], F32)
        W2.r_rsq = Res("rsqb")
        W2.xin = Ring(S, "xinb", 8, [128, TT], F32)
        W3 = Work()
        W3.__dict__.update(W.__dict__)
        W3.xin = W2.xin
        Ws = [W2, W3, W2, W]
        wo = Ring(S, "wo", 3, [128, KC * 128], BF16)
        r_mixc = [Res(f"mixc{m}") for m in range(KC)]
        if from_zs:
            hbr = Ring(S, "hbz", 2, [128, KC, TT], BF16)
        posts = None
        hbq = {}

        def load_hbz(t):
            if from_zs and t < NT:
                hb, r_hb1, dn = hbr.next()
                S.dma("sp", dn, lambda e: e.dma_start(
                    out=hb[:], in_=zs[:, :, t * TT:(t + 1) * TT].rearrange("c p t -> p c t")),
                    reads=r_zs, writes=[r_hb1])
                hbq[t] = (hb, r_hb1)
        load_hbz(0)
        for t in range(NT):
            t0 = t * TT
            Wt = Ws[t]
            load_hbz(t + 1)
            if from_zs:
                hb, r_hb1 = hbq.pop(t)
                rhs = lambda k, hb=hb: hb[:, k, :]
                rr = [r_hb1]
            else:
                rhs = lambda k, t0=t0: hx[:, k, t0:t0 + TT]
                rr = r_hx
            if t == NT - 1 and next_pre is not None:
                posts = list(posts) + next_pre()
            if t == 0:
                y_phase(Wt, lambda m: wsrc[m], KC, rhs, rr, TT, wo, inter=posts,
                        cache=(lambda m: mixc[l, m], r_mixc, True))
            else:
                y_phase(Wt, lambda m: mixc[l, m], KC, rhs, rr, TT, wo, inter=posts, cache=(None, r_mixc, False))
            posts = post_steps(Wt, xs, r_xs[t], dst, r_dst[t], t0, TT, l, 1, 0)
        return posts

    def mixer_pre_work():
        W = Work()
        W.__dict__.update(WC.__dict__)
        W.xinA = Ring(S, "xinB", 10, [128, TT], F32)
        W.rs2 = S.sbuf("rs2", [128, TT], F32)
        W.r_rs2 = Res("rs2")
        return W

    def rglru(carry=None, next_pre=None):
        l = 0
        S.soft_switch()
        m0 = S.mark()
        top0 = S.sb_top
        hx = S.sbuf_top("hx", [128, KC, SEQ], BF16)
        r_hx = [Res(f"hx{k}") for k in range(KC)]
        hcb = S.sbuf_top("hcb", [128, KC, CTX], BF16)
        r_hcb = [Res(f"hcb{k}") for k in range(KC)]
        W = mixer_pre_work()
        r_hxt = [[Res(f"hx{t}_{k}") for k in range(KC)] for t in range(NT)]
        pre_pipeline(W, [(xs, r_xs[t], t * TT, TT, l, 1, 0,
                          (lambda kc, t=t: hx[:, kc, t * TT:(t + 1) * TT]), (lambda kc, t=t: r_hxt[t][kc]))
                         for t in range(NT)] +
                     [(ctxs, r_ctxs[0], 0, CTX, l, 1, 1, (lambda kc: hcb[:, kc, :]), (lambda kc: r_hcb[kc]))],
                     carry=carry)
        S.barrier()
        S.reset(M_BASE)
        wi = Ring(S, "wi", 2, [128, 16 * 256], BF16)
        gwr = Ring(S, "gwr", 3, [128, 4 * 128], BF16)
        vp = S.sbuf("vp", [128, 32, 67], F32)
        vc = S.sbuf("vc", [128, SEQ], F32)
        vcb = S.sbuf("vcb", [128, SEQ], BF16)
        bA = S.sbuf("bA", [128, SEQ], F32)
        bB = [S.sbuf(f"bB{d}", [128, SEQ], F32) for d in range(2)]
        bC = [S.sbuf(f"bC{d}", [128, SEQ], F32) for d in range(2)]
        bH = [S.sbuf(f"bH{d}", [128, SEQ], F32) for d in range(2)]
        bG = [S.sbuf(f"bG{i}", [128, SEQ], F32) for i in range(2)]
        zb = Ring(S, "zb", 1, [128, SEQ], BF16)
        vpc = [S.sbuf(f"vpc{i}", [128, CTX + 3], F32) for i in range(2)]
        vcc = [S.sbuf(f"vcc{i}", [128, CTX], F32) for i in range(2)]
        vccb = [S.sbuf(f"vccb{i}", [128, CTX], BF16) for i in range(2)]
        cA = S.sbuf("cA", [128, CTX], F32)
        cB = [S.sbuf(f"cB{d}", [128, CTX], F32) for d in range(2)]
        cC = [S.sbuf(f"cC{d}", [128, CTX], F32) for d in range(2)]
        cH = S.sbuf("cH", [128, 2, CTX], F32)
        hgb = S.sbuf("hgb", [128, 64], F32)
        r_vp, r_vc, r_vcb = Res(), Res(), Res()
        r_G = [Res(), Res()]
        HALF = SEQ // 2
        r_A = [Res(), Res()]
        r_B = [[Res(), Res()] for d in range(2)]
        r_C = [Res(), Res()]
        r_H = [Res(), Res()]
        r_vpc, r_vcc, r_vccb = [Res(), Res()], [Res(), Res()], [Res(), Res()]
        r_cA, r_cB, r_cC, r_cH = Res(), [Res(), Res()], [Res(), Res()], [Res(), Res()]
        S.op("dve", lambda e: e.memset(vp[:], 0.0), writes=[r_vp])
        for i in range(2):
            S.op("dve", lambda e, i=i: e.memset(vpc[i][:], 0.0), writes=[r_vpc[i]])
        S.op("act", lambda e: e.activation(out=coef[:, 0:32], in_=vecs[:, V_LAM:V_LAM + 32], func=AF.Exp, scale=-1.0),
             reads=[r_vecs], writes=[r_coef])
        S.op("act", lambda e: e.activation(out=coef[:, 0:32], in_=coef[:, 0:32], func=AF.Ln, bias=oneb[:], scale=1.0),
             reads=[r_coef, r_const], writes=[r_coef])
        S.op("dve", lambda e: e.tensor_scalar(out=coef[:, 32:64], in0=coef[:, 0:32], scalar1=-4.0, scalar2=None,
                                              op0=ALU.mult), reads=[r_coef], writes=[r_coef])
        S.op("dve", lambda e: e.tensor_scalar(out=coef[:, 0:32], in0=coef[:, 0:32], scalar1=-8.0, scalar2=None,
                                              op0=ALU.mult), reads=[r_coef], writes=[r_coef])
        S.op("dve", lambda e: e.tensor_scalar(out=hgb[:], in0=vecs[:, V_GB:V_GB + 64], scalar1=0.5, scalar2=None,
                                              op0=ALU.mult), reads=[r_vecs], writes=[r_coef])
        pv3 = PA[:, :].rearrange("p (r w) -> p r w", w=64)
        vc3 = vc[:, :].rearrange("p (r w) -> p r w", w=64)

        def tanh_pair(d, c, ps_r, rps_r, ps_i, rps_i, A, rA, B, rB):
            S.op("act", lambda e: e.activation(out=A, in_=ps_r, func=AF.Tanh, scale=0.5,
                                               bias=hgb[:, (d * 2 + 0) * 16 + c:(d * 2 + 0) * 16 + c + 1]),
                 reads=rps_r + [r_coef], writes=rA)
            S.op("act", lambda e: e.activation(out=B, in_=ps_i, func=AF.Tanh, scale=0.5,
                                               bias=hgb[:, (d * 2 + 1) * 16 + c:(d * 2 + 1) * 16 + c + 1]),
                 reads=rps_i + [r_coef], writes=rB)

        wslots = {}

        def load_w(c):
            if c >= KC:
                return
            buf, rb, dn = wi.next()
            S.dma("pool", dn, lambda e: e.dma_start(out=buf[:], in_=recin[c], max_dma_last_dim=8192), writes=[rb])
            gb, rgb, gdn = gwr.next()
            S.dma("pool", gdn, lambda e: e.dma_start(out=gb[:], in_=gw[c]), writes=[rgb])
            wslots[c] = (buf, rb, gb, rgb)

        def emit_ctxv(c):
            if c >= KC:
                return
            buf, rb, gb, rgb = wslots[c]
            p = c % 2
            S.op("pe", lambda e: mm_group(e, bank(PA, 0, CTX), [(buf[:, k * 256 + 128:k * 256 + 256], hcb[:, k, :])
                                                                for k in range(KC)]),
                 reads=[rb] + r_hcb, writes=[rPA[0]])
            S.op("act", lambda e: e.activation(out=vpc[p][:, 2:2 + CTX], in_=bank(PA, 0, CTX), func=AF.Copy),
                 reads=[rPA[0]], writes=[r_vpc[p]])

        def emit_v(c):
            if c >= KC:
                return
            buf, rb, gb, rgb = wslots[c]
            for tt in range(NT):
                S.op("pe", lambda e, tt=tt: mm_group(e, bank(PA, tt), [
                    (buf[:, k * 256 + 128:k * 256 + 256], hx[:, k, tt * TT:(tt + 1) * TT]) for k in range(KC)]),
                    reads=[rb] + r_hx, writes=[rPA[tt]])

        def stage1(c):
            if c >= KC:
                return
            buf, rb, gb, rgb = wslots[c]
            p = c % 2
            S.op("act", lambda e: e.activation(out=vp[:, :, 2:66], in_=pv3, func=AF.Copy), reads=rPA, writes=[r_vp])
            for tt in range(NT):
                S.op("pe", lambda e, tt=tt: mm_group(e, bank(PA, tt), [
                    (buf[:, k * 256:k * 256 + 128], hx[:, k, tt * TT:(tt + 1) * TT]) for k in range(KC)]),
                    reads=[rb] + r_hx, writes=[rPA[tt]])
            load_w(c + 2)
            S.op("act", lambda e: e.activation(out=bG[p][:], in_=PA[:, :], func=AF.Gelu_apprx_tanh),
                 reads=rPA, writes=[r_G[p]])
            cw = lambda k: vcol(V_CW, k * 16 + c)
            S.op("dve", lambda e: e.tensor_scalar(out=vcc[p][:], in0=vpc[p][:, 2:2 + CTX], scalar1=cw(2),
                                                  scalar2=vcol(V_CB, c), op0=ALU.mult, op1=ALU.add),
                 reads=[r_vpc[p], r_vecs], writes=[r_vcc[p]])
            for k, o in ((0, 0), (1, 1), (3, 3)):
                S.op("dve", lambda e, k=k, o=o: e.scalar_tensor_tensor(
                    out=vcc[p][:], in0=vpc[p][:, o:o + CTX], scalar=cw(k), in1=vcc[p][:], op0=ALU.mult, op1=ALU.add),
                    reads=[r_vpc[p], r_vcc[p], r_vecs], writes=[r_vcc[p]])
            S.op("dve", lambda e: e.tensor_copy(out=vccb[p][:], in_=vcc[p][:]), reads=[r_vcc[p]], writes=[r_vccb[p]])
            S.op("dve", lambda e: e.tensor_scalar(out=vc3, in0=vp[:, :, 2:66], scalar1=cw(2),
                                                  scalar2=vcol(V_CB, c), op0=ALU.mult, op1=ALU.add),
                 reads=[r_vp, r_vecs], writes=[r_vc])
            for k, o in ((0, 0), (1, 1), (3, 3)):
                S.op("dve", lambda e, k=k, o=o: e.scalar_tensor_tensor(
                    out=vc3, in0=vp[:, :, o:o + 64], scalar=cw(k), in1=vc3, op0=ALU.mult, op1=ALU.add),
                    reads=[r_vp, r_vc, r_vecs], writes=[r_vc])
            S.op("dve", lambda e: e.tensor_copy(out=vcb[:], in_=vc[:]), reads=[r_vc], writes=[r_vcb])
            emit_ctxv(c + 1)

        load_w(0)
        load_w(1)
        emit_ctxv(0)
        emit_v(0)
        stage1(0)
        pct = preconv_tasks(0, 1) if nsub >= 3 else []
        n_pct = len(pct)
        for c in range(KC):
            buf, rb, gb, rgb = wslots[c]
            p = c % 2
            while len(pct) > n_pct - (n_pct * (c + 1)) // KC:
                pct.pop(0)()
            gcol = lambda d, g: hgb[:, (d * 2 + g) * 16 + c:(d * 2 + g) * 16 + c + 1]
            chc = lambda d: coef[:, 32 + d * 16 + c:32 + d * 16 + c + 1]
            def pe_c(e):
                ins = None
                for d in range(2):
                    e.matmul(PB[:, d * 512:d * 512 + CTX], gb[:, (d * 2) * 128:(d * 2 + 1) * 128], vccb[p][:],
                             start=True, stop=True)
                    ins = e.matmul(PB[:, d * 512 + CTX:d * 512 + 2 * CTX], gb[:, (d * 2 + 1) * 128:(d * 2 + 2) * 128],
                                   vccb[p][:], start=True, stop=True)
                return ins
            S.op("pe", pe_c, reads=[rgb, r_vccb[p]], writes=[rPB[0], rPB[1]])
            for d in range(2):
                S.op("act", lambda e, d=d: e.activation(out=cA[:], in_=PB[:, d * 512:d * 512 + CTX], func=AF.Tanh,
                                                        scale=0.5, bias=gcol(d, 0)),
                     reads=[rPB[d], r_coef], writes=[r_cA])
                S.op("act", lambda e, d=d: e.activation(out=cB[d][:], in_=PB[:, d * 512 + CTX:d * 512 + 2 * CTX],
                                                        func=AF.Tanh, scale=0.5, bias=gcol(d, 1)),
                     reads=[rPB[d], r_coef], writes=[r_cB[d]])
                S.op("act", lambda e, d=d: e.activation(out=cC[d][:], in_=cA[:], func=AF.Exp, scale=chc(d), bias=chc(d)),
                     reads=[r_cA, r_coef], writes=[r_cC[d]])
            for d in range(2):
                gwa = gb[:, (d * 2 + 0) * 128:(d * 2 + 1) * 128]
                gwx = gb[:, (d * 2 + 1) * 128:(d * 2 + 2) * 128]
                for hf in range(2):
                    lo, hi = hf * HALF, (hf + 1) * HALF

                    def pe_l(e, gwa=gwa, gwx=gwx, lo=lo):
                        for t2 in range(2):
                            e.matmul(bank(PB, t2), gwa, vcb[:, lo + t2 * TT:lo + (t2 + 1) * TT], start=True, stop=True)
                        ins = None
                        for t2 in range(2):
                            ins = e.matmul(bank(PB, 2 + t2), gwx, vcb[:, lo + t2 * TT:lo + (t2 + 1) * TT],
                                           start=True, stop=True)
                        return ins
                    S.op("pe", pe_l, reads=[rgb, r_vcb], writes=rPB)
                    S.op("act", lambda e, d=d, lo=lo, hi=hi: e.activation(
                        out=bA[:, lo:hi], in_=PB[:, 0:HALF], func=AF.Tanh, scale=0.5, bias=gcol(d, 0)),
                        reads=rPB[0:2] + [r_coef], writes=[r_A[hf]])
                    S.op("act", lambda e, d=d, lo=lo, hi=hi: e.activation(
                        out=bB[d][:, lo:hi], in_=PB[:, HALF:SEQ], func=AF.Tanh, scale=0.5, bias=gcol(d, 1)),
                        reads=rPB[2:4] + [r_coef], writes=[r_B[d][hf]])
                S.op("act", lambda e, d=d: e.activation(out=bC[d][:], in_=bA[:], func=AF.Exp, scale=chc(d), bias=chc(d)),
                     reads=r_A + [r_coef], writes=[r_C[d]])
            emit_v(c + 1)
            for d in range(2):
                S.op("dve", lambda e, d=d: e.tensor_tensor(out=cH[:, d, :], in0=cC[d][:], in1=cC[d][:], op=ALU.mult),
                     reads=[r_cC[d]], writes=[r_cH[d]])
                S.op("dve", lambda e, d=d: e.tensor_tensor(out=bH[d][:], in0=bC[d][:], in1=bC[d][:], op=ALU.mult),
                     reads=[r_C[d]], writes=[r_H[d]])
            for d in range(2):
                S.op("act", lambda e, d=d: e.activation(out=cH[:, d, :], in_=cH[:, d, :], func=AF.Sqrt, bias=oneb[:],
                                                        scale=-1.0), reads=[r_cH[d], r_const], writes=[r_cH[d]])
                S.op("act", lambda e, d=d: e.activation(out=bH[d][:], in_=bH[d][:], func=AF.Sqrt, bias=oneb[:],
                                                        scale=-1.0), reads=[r_H[d], r_const], writes=[r_H[d]])
            for d in range(2):
                S.op("dve", lambda e, d=d: e.scalar_tensor_tensor(out=cB[d][:], in0=cB[d][:], scalar=1.0, in1=cH[:, d, :],
                                                                  op0=ALU.add, op1=ALU.mult),
                     reads=[r_cH[d], r_cB[d]], writes=[r_cB[d]])
                S.op("dve", lambda e, d=d: e.scalar_tensor_tensor(out=cB[d][:], in0=cB[d][:], scalar=0.5, in1=vcc[p][:],
                                                                  op0=ALU.mult, op1=ALU.mult),
                     reads=[r_cB[d], r_vcc[p]], writes=[r_cB[d]])
                S.op("dve", lambda e, d=d: e.scalar_tensor_tensor(out=bB[d][:], in0=bB[d][:], scalar=1.0, in1=bH[d][:],
                                                                  op0=ALU.add, op1=ALU.mult),
                     reads=[r_H[d]] + r_B[d], writes=r_B[d])
                S.op("dve", lambda e, d=d: e.scalar_tensor_tensor(out=bB[d][:], in0=bB[d][:], scalar=0.5, in1=vc[:],
                                                                  op0=ALU.mult, op1=ALU.mult),
                     reads=r_B[d] + [r_vc], writes=r_B[d])
            stage1(c + 1)
            S.op("dve", lambda e: e.tensor_tensor_scan(out=cH[:, 0, :], data0=cC[0][:], data1=cB[0][:],
                                                       initial=0.0, op0=ALU.mult, op1=ALU.add),
                 reads=[r_cC[0], r_cB[0]], writes=[r_cH[0]])
            S.op("dve", lambda e: e.tensor_tensor_scan(
                out=bH[0][:], data0=bC[0][:], data1=bB[0][:], initial=cH[:, 0, CTX - 1:CTX], op0=ALU.mult, op1=ALU.add),
                reads=[r_C[0]] + r_B[0] + [r_cH[0]], writes=[r_H[0]])
            S.op("dve", lambda e: e.tensor_tensor_scan(out=rev_ap(cH[:, 1, :]), data0=rev_ap(cC[1][:]),
                                                       data1=rev_ap(cB[1][:]), initial=0.0,
                                                       op0=ALU.mult, op1=ALU.add),
                 reads=[r_cC[1], r_cB[1]], writes=[r_cH[1]])
            S.op("dve", lambda e: e.tensor_tensor_scan(
                out=rev_ap(bH[1][:]), data0=rev_ap(bC[1][:]), data1=rev_ap(bB[1][:]), initial=cH[:, 1, 0:1],
                op0=ALU.mult, op1=ALU.add), reads=[r_C[1]] + r_B[1] + [r_cH[1]], writes=[r_H[1]])
            S.op("dve", lambda e: e.tensor_tensor(out=bH[0][:], in0=bH[0][:], in1=bH[1][:], op=ALU.add),
                 reads=r_H, writes=[r_H[0]])
            zt, rz, zdn = zb.next()
            S.op("dve", lambda e, zt=zt: e.tensor_tensor(out=zt[:], in0=bH[0][:], in1=bG[p][:], op=ALU.mult),
                 reads=[r_H[0], r_G[p]], writes=[rz])
            S.dma("sp", zdn, lambda e, zt=zt, c=c: e.dma_start(out=zs[c], in_=zt[:]), reads=[rz], writes=[r_zs[c]])
        S.barrier()
        S.reset(m0)
        S.sb_top = top0
        posts = mixer_out(0, recout, True, dst=xs, r_dst=r_xs, next_pre=next_pre)
        S.reset(m0)
        return posts

    def fourier(carry=None, next_pre=None):
        l = 1
        S.soft_switch()
        m0 = S.mark()
        top0 = S.sb_top
        hx = S.sbuf_top("hxf", [128, KC, SEQ], BF16)
        r_hx = [Res(f"hxf{k}") for k in range(KC)]
        W = mixer_pre_work()
        r_hxt = [[Res(f"hxf{t}_{k}") for k in range(KC)] for t in range(NT)]
        pre_pipeline(W, [(xs, r_xs[t], t * TT, TT, l, 1, 0,
                          (lambda kc, t=t: hx[:, kc, t * TT:(t + 1) * TT]), (lambda kc, t=t: r_hxt[t][kc]))
                         for t in range(NT)], carry=carry)
        S.barrier()
        S.reset(M_BASE)
        cs = S.sbuf("cs", [128, 2 * 512], BF16)
        r_cs = Res()
        S.dma("sp", "cs", lambda e: e.dma_start(out=cs[:], in_=cs_d), writes=[r_cs])
        xcs = Ring(S, "xcs", 4, [128, KC, 512], BF16)
        dbuf = Ring(S, "dbuf", 4, [128, 16 * 512], BF16)
        pct = preconv_tasks(1, 1) if nsub >= 6 else []
        n_pct = len(pct)
        for gp in range(4):
            xbs = []
            for g in (2 * gp, 2 * gp + 1):
                xb, rxb, _ = xcs.next()
                xbs.append((g, xb, rxb))
                for n in range(KC):
                    pb = n % 2
                    S.op("pe", lambda e, n=n, pb=pb, g=g: mm_group(e, bank(PA, pb), [
                        (hx[:, 2 * g + jj, n * 128:(n + 1) * 128], cs[:, jj * 512:(jj + 1) * 512])
                        for jj in range(2)]),
                        reads=[r_hx[2 * g], r_hx[2 * g + 1], r_cs], writes=[rPA[pb]])
                    S.op("act" if n % 2 == 0 else "dve",
                         (lambda e, n=n, pb=pb, xb=xb: e.activation(out=xb[:, n, :], in_=bank(PA, pb), func=AF.Copy))
                         if n % 2 == 0 else
                         (lambda e, n=n, pb=pb, xb=xb: e.tensor_copy(out=xb[:, n, :], in_=bank(PA, pb))),
                         reads=[rPA[pb]], writes=[rxb])
            for kt in range(NT):
                dc, rdc, dnc = dbuf.next()
                S.dma("pool", dnc, lambda e, dc=dc, kt=kt: e.dma_start(out=dc[:], in_=dftn[kt, 0]), writes=[rdc])
                ds_, rds, dns = dbuf.next()
                S.dma("pool", dns, lambda e, ds_=ds_, kt=kt: e.dma_start(out=ds_[:], in_=dftn[kt, 1]), writes=[rds])
                while len(pct) > n_pct - (n_pct * (gp * NT + kt + 1)) // (4 * NT):
                    pct.pop(0)()
                i2 = 0
                for (g, xb, rxb) in xbs:
                    for ff in range(2):
                        pb = 2 + i2 % 2
                        i2 += 1
                        S.op("pe", lambda e, ff=ff, pb=pb, xb=xb, dc=dc, ds_=ds_: mm_group(
                            e, bank(PA, pb),
                            [(xb[:, n, ff * 128:(ff + 1) * 128], dc[:, n * 512:(n + 1) * 512]) for n in range(KC)] +
                            [(xb[:, n, 256 + ff * 128:256 + (ff + 1) * 128], ds_[:, n * 512:(n + 1) * 512])
                             for n in range(KC)]),
                            reads=[rxb, rdc, rds], writes=[rPA[pb]])
                        S.op("act", lambda e, ff=ff, pb=pb, kt=kt, g=g: e.activation(
                            out=hx[:, 2 * g + ff, kt * TT:(kt + 1) * TT], in_=bank(PA, pb), func=AF.Copy),
                            reads=[rPA[pb]], writes=[r_hx[2 * g + ff]])
        S.barrier()
        S.reset(m0)
        posts = mixer_out(1, fouout, False, hx=hx, r_hx=r_hx, dst=xs, r_dst=r_xs, next_pre=next_pre)
        S.reset(m0)
        S.sb_top = top0
        return posts

    def xjobs(src, r_src, dst, r_dst):
        return [(src, r_src, dst, r_dst, t * TT, TT, t, 0) for t in range(NT)]

    adaln_start()
    final = lambda k: (outT, r_out) if nsub == k else (xs, r_xs)
    d_, rd_ = final(1)
    carry = ffn(0, 0, xjobs(xT, r_xT, d_, rd_) + [(ctxT, r_ctxT, ctxs, r_ctxs, 0, CTX, 0, 1)],
                bg_l=0, bg_idxs=range(48, 144), defer=True)
    adaln_evac(0, 48, 144)
    der_compute(0, 1)
    der_compute(0, 2)
    if nsub >= 2:
        d3, rd3 = final(3)
        j3 = xjobs(xs, r_xs, d3, rd3)
        carry = rglru(carry, next_pre=(lambda: ffn_first_pre(0, 1, j3[0])) if nsub >= 3 else None)
    if nsub >= 3:
        d4, rd4 = final(4)
        j4 = xjobs(xs, r_xs, d4, rd4)

        def l1_mods():
            adaln_evac(1, 0, 144)
            for s_ in range(3):
                der_compute(1, s_)
        carry = ffn(0, 1, j3, bg_l=1, bg_idxs=range(0, 144), carry=carry, defer=True, pre_done=True,
                    before_last_y=l1_mods, next_pre=(lambda: ffn_first_pre(1, 0, j4[0])) if nsub >= 4 else None)
    if nsub >= 4:
        carry = ffn(1, 0, j4, carry=carry, defer=True, pre_done=True)
    if nsub >= 5:
        d6, rd6 = final(6)
        j6 = xjobs(xs, r_xs, d6, rd6)
        carry = fourier(carry, next_pre=(lambda: ffn_first_pre(1, 1, j6[0])) if nsub >= 6 else None)
    if nsub >= 6:
        carry = ffn(1, 1, j6, carry=carry, defer=True, pre_done=True)
    for st in carry:
        st()
    S.barrier()
    if nsub in (2, 5):
        m0 = S.mark()
        cp = Ring(S, "cp", 2, [128, SEQ], F32)
        for m in range(KC):
            b, rb, dn = cp.next()
            S.dma("sp", dn, lambda e, b=b, m=m: e.dma_start(out=b[:], in_=xs[m]),
                  reads=[r_xs[t][m] for t in range(NT)], writes=[rb])
            S.dma("sp", dn + "o", lambda e, b=b, m=m: e.dma_start(out=outT[m], in_=b[:]), reads=[rb],
                  writes=[r_out[t][m] for t in range(NT)])
        S.reset(m0)
    S.barrier()
    S.close()
    return nc


def _fm(v):
    v = np.asarray(v, np.float32)
    lead = v.shape[:-1]
    return np.moveaxis(v.reshape(lead + (KC, 128)), -1, 0)


def prep_shared(inp):
    f32 = np.float32
    mod_w = np.asarray(inp["mod_w"], f32)
    modw_t = np.ascontiguousarray(mod_w.reshape(2, KC, 128, 144, 128).transpose(0, 3, 2, 1, 4)).reshape(2, 144, 128, 16 * 128)
    w_in = np.asarray(inp["ffn_w_in"], f32)
    win_t = np.ascontiguousarray(w_in.reshape(2, 2, KC, 128, 2, JC, 128).transpose(0, 1, 5, 3, 2, 4, 6)).reshape(2, 2, JC, 128, 16 * 256)
    w_out = np.asarray(inp["ffn_w_out"], f32)
    wout_t = np.ascontiguousarray(w_out.reshape(2, 2, JC, 128, KC, 128).transpose(0, 1, 4, 3, 2, 5)).reshape(2, 2, KC, 128, JC * 128)
    r_in = np.asarray(inp["rec_w_in"], f32)[0]
    recin_t = np.ascontiguousarray(r_in.reshape(KC, 128, 2, KC, 128).transpose(3, 1, 0, 2, 4)).reshape(KC, 128, 16 * 256)
    r_out = np.asarray(inp["rec_w_out"], f32)[0]
    recout_t = np.ascontiguousarray(r_out.reshape(KC, 128, KC, 128).transpose(2, 1, 0, 3)).reshape(KC, 128, 16 * 128)
    f_out = np.asarray(inp["fou_w_out"], f32)[0]
    fouout_t = np.ascontiguousarray(f_out.reshape(KC, 128, KC, 128).transpose(2, 1, 0, 3)).reshape(KC, 128, 16 * 128)
    g_w = np.asarray(inp["rec_gate_w"], f32)[0]
    gw_t = np.ascontiguousarray(g_w.transpose(2, 3, 0, 1, 4)).reshape(KC, 128, 4 * 128)
    j = np.arange(256)
    ang = 2 * np.pi * ((j[:, None] * j[None, :]) % 256) / 256.0
    cc, sc = np.cos(ang) / 16.0, np.sin(ang) / 16.0
    cs = np.concatenate([cc, sc], axis=1).reshape(2, 128, 512).transpose(1, 0, 2).reshape(128, 1024)
    n = np.arange(SEQ)
    angn = 2 * np.pi * ((n[:, None] * n[None, :]) % SEQ) / float(SEQ)
    sq = 1.0 / math.sqrt(SEQ)
    cn, sn = np.cos(angn) * sq, -np.sin(angn) * sq
    dft = np.stack([cn, sn], 0).reshape(2, KC, 128, NT, 512).transpose(3, 0, 2, 1, 4).reshape(NT, 2, 128, 16 * 512)
    bf = ml_dtypes.bfloat16
    return dict(modw=modw_t, win=win_t, wout=wout_t, recin=recin_t, recout=recout_t, fouout=fouout_t, gw=gw_t,
                cs=np.ascontiguousarray(cs).astype(bf), dftn=np.ascontiguousarray(dft).astype(bf))


def prep_core(inp, b):
    f32 = np.float32
    x = np.asarray(inp["x"], f32)[b]
    ctx = np.asarray(inp["ctx"], f32)[b]
    vecs = np.zeros((128, NV), f32)
    cc = np.stack([np.asarray(inp["c"], f32)[b], np.asarray(inp["c_ctx"], f32)], 0)
    vecs[:, V_C:V_C + 32] = _fm(cc).transpose(0, 2, 1).reshape(128, 32)
    vecs[:, V_MODB:V_MODB + 288] = _fm(np.asarray(inp["mod_b"], f32).reshape(2, 9, D)).reshape(128, 288)
    vecs[:, V_NG:V_NG + 192] = _fm(np.asarray(inp["norm_g"], f32)).reshape(128, 192)
    vecs[:, V_GB:V_GB + 64] = _fm(np.asarray(inp["rec_gate_b"], f32)[0]).reshape(128, 64)
    vecs[:, V_LAM:V_LAM + 32] = _fm(np.asarray(inp["rec_lam"], f32)[0]).reshape(128, 32)
    vecs[:, V_CW:V_CW + 64] = _fm(np.asarray(inp["rec_conv_w"], f32)[0]).reshape(128, 64)
    vecs[:, V_CB:V_CB + 16] = _fm(np.asarray(inp["rec_conv_b"], f32)[0]).reshape(128, 16)
    return dict(xT=np.ascontiguousarray(x.T).reshape(KC, 128, SEQ),
                ctxT=np.ascontiguousarray(ctx.T).reshape(KC, 128, CTX), vecs=vecs)


_NC_CACHE = {}


def run(inputs, nsub=6, cores=8):
    if nsub not in _NC_CACHE:
        _NC_CACHE[nsub] = build(nsub)
    nc = _NC_CACHE[nsub]
    import time as _t
    _t0 = _t.time()
    shared = prep_shared(inputs)
    in_maps = []
    for b in range(cores):
        m = dict(shared)
        m.update(prep_core(inputs, b))
        in_maps.append(m)
    _t1 = _t.time()
    res = run_bass_kernel_spmd(nc, in_maps, core_ids=list(range(cores)))
    print(f"[kernel] prep {_t1 - _t0:.1f}s  spmd {_t.time() - _t1:.1f}s", flush=True)
    out = np.stack([np.ascontiguousarray(r["outT"].reshape(D, SEQ).T) for r in res.results], 0)
    return out.astype(np.float32)


def kernel(**inputs):
    return run(inputs, 6, 8)
```
